# Optimizing a Trainium2 kernel written in Bass

```python
import math
import jax, jax.numpy as jnp
from jax import lax
import numpy as np

D_MODEL = 2048
BATCH = 2
SEQ = 4096
DEPTH = 4

N_MIXERS = 2
N_ATTN_LAYERS = (DEPTH + 1) // 2
N_SSM_LAYERS = DEPTH // 2
EPS = 1e-6
ROPE_THETA = 10000.0

ATTN_HEADS = 16
ATTN_HEAD_DIM = D_MODEL // ATTN_HEADS // 2
ATTN_V_DIM = 2 * ATTN_HEAD_DIM
ATTN_QK_WIDTH = ATTN_HEADS * 2 * ATTN_HEAD_DIM
ATTN_WIDTH = ATTN_HEADS * ATTN_V_DIM
ATTN_IN = 2 * ATTN_QK_WIDTH + 2 * ATTN_WIDTH
Q_BLOCK = 128

SSM_EXPAND = 2
SSM_D_INNER = SSM_EXPAND * D_MODEL
SSM_HEAD_DIM = 64
SSM_HEADS = SSM_D_INNER // SSM_HEAD_DIM
SSM_GROUPS = 8
SSM_STATE = 128
SSM_CONV = 4
SSM_CHUNK = 128
SSM_CONV_DIM = SSM_D_INNER + 2 * SSM_GROUPS * SSM_STATE
SSM_IN = SSM_D_INNER + SSM_CONV_DIM + SSM_HEADS

kernel_name = "hybrid_diffattn_ssd_adaln_trunk"


def rms_norm(x, w, eps=EPS):
    xf = x.astype(jnp.float32)
    y = xf * lax.rsqrt(jnp.mean(xf * xf, axis=-1, keepdims=True) + eps)
    return (y * w.astype(jnp.float32)).astype(x.dtype)


def rope(x, positions):
    d = x.shape[-1]
    inv_freq = ROPE_THETA ** (-jnp.arange(0, d, 2, dtype=jnp.float32) / d)
    ang = positions.astype(jnp.float32)[..., None] * inv_freq
    cos = jnp.cos(ang)[:, :, None, None, :]
    sin = jnp.sin(ang)[:, :, None, None, :]
    x1, x2 = jnp.split(x.astype(jnp.float32), 2, axis=-1)
    return jnp.concatenate([x1 * cos - x2 * sin, x2 * cos + x1 * sin], axis=-1).astype(x.dtype)


def diff_attention(h, positions, w_in, q_norm, k_norm, lq1, lk1, lq2, lk2,
                   subln_w, w_out, lambda_init):
    b, s, _ = h.shape
    H, dh, dv = ATTN_HEADS, ATTN_HEAD_DIM, ATTN_V_DIM
    nb = s // Q_BLOCK
    proj = h @ w_in
    q, k, v, g = jnp.split(
        proj, [ATTN_QK_WIDTH, 2 * ATTN_QK_WIDTH, 2 * ATTN_QK_WIDTH + ATTN_WIDTH], axis=-1)
    q = rope(rms_norm(q.reshape(b, s, H, 2, dh), q_norm), positions)
    k = rope(rms_norm(k.reshape(b, s, H, 2, dh), k_norm), positions)
    v = v.reshape(b, s, H, dv)
    lam = (jnp.exp(jnp.sum(lq1.astype(jnp.float32) * lk1.astype(jnp.float32)))
           - jnp.exp(jnp.sum(lq2.astype(jnp.float32) * lk2.astype(jnp.float32)))
           + lambda_init)
    qb = q.reshape(b, nb, Q_BLOCK, H, 2, dh).transpose(1, 0, 3, 4, 2, 5)
    kt = k.transpose(0, 2, 3, 1, 4)
    vt = v.transpose(0, 2, 1, 3)
    key_idx = jnp.arange(s)
    scale = dh ** -0.5

    def block(args):
        i, qi = args
        sc = jnp.einsum('bhmqd,bhmkd->bhmqk', qi, kt).astype(jnp.float32) * scale
        q_idx = i * Q_BLOCK + jnp.arange(Q_BLOCK)
        causal = key_idx[None, :] <= q_idx[:, None]
        p = jax.nn.softmax(jnp.where(causal, sc, -jnp.inf), axis=-1)
        pd = p[:, :, 0] - lam * p[:, :, 1]
        return jnp.einsum('bhqk,bhkv->bhqv', pd.astype(vt.dtype), vt)

    o = lax.map(block, (jnp.arange(nb), qb))
    o = o.transpose(1, 0, 3, 2, 4).reshape(b, s, H, dv)
    o = rms_norm(o, subln_w) * (1.0 - lambda_init)
    o = o.reshape(b, s, ATTN_WIDTH) * jax.nn.silu(g)
    return (o @ w_out).astype(h.dtype)


def causal_depthwise_conv(x, w, bias):
    y = lax.conv_general_dilated(
        x, w[:, None, :].astype(x.dtype), window_strides=(1,), padding=[(SSM_CONV - 1, 0)],
        dimension_numbers=('NWC', 'WIO', 'NWC'), feature_group_count=x.shape[-1])
    return y + bias


def ssd_mixer(h, w_in, conv_w, conv_b, dt_bias, A_log, D_skip, norm_w, w_out):
    b, s, _ = h.shape
    G, R, P, N, L = SSM_GROUPS, SSM_HEADS // SSM_GROUPS, SSM_HEAD_DIM, SSM_STATE, SSM_CHUNK
    nc = s // L
    proj = h @ w_in
    z, xbc, dt = jnp.split(proj, [SSM_D_INNER, SSM_D_INNER + SSM_CONV_DIM], axis=-1)
    xbc = jax.nn.silu(causal_depthwise_conv(xbc, conv_w, conv_b))
    xs, Bm, Cm = jnp.split(xbc, [SSM_D_INNER, SSM_D_INNER + G * N], axis=-1)
    dt = jax.nn.softplus((dt + dt_bias).astype(jnp.float32))
    A = -jnp.exp(A_log.astype(jnp.float32))
    xs = xs.reshape(b, nc, L, G, R, P)
    Bm = Bm.reshape(b, nc, L, G, N)
    Cm = Cm.reshape(b, nc, L, G, N)
    dt_c = dt.reshape(b, nc, L, G, R)
    xdt = xs * dt_c[..., None]
    dA_cs = jnp.cumsum((dt_c * A.reshape(G, R)).transpose(0, 3, 4, 1, 2), axis=-1)
    t_idx = jnp.arange(L)
    causal = t_idx[:, None] >= t_idx[None, :]
    seg = dA_cs[..., :, None] - dA_cs[..., None, :]
    decay = jnp.exp(jnp.where(causal, seg, -jnp.inf))
    CB = jnp.einsum('bclgn,bcsgn->bgcls', Cm, Bm)
    y_diag = jnp.einsum('bgcls,bgrcls,bcsgrp->bclgrp', CB, decay, xdt)
    decay_states = jnp.exp(dA_cs[..., -1:] - dA_cs)
    states = jnp.einsum('bclgn,bgrcl,bclgrp->cbgrpn', Bm, decay_states, xdt)
    chunk_decay = jnp.exp(dA_cs[..., -1]).transpose(3, 0, 1, 2)

    def step(hstate, inp):
        st, dec = inp
        return hstate * dec[..., None, None] + st, hstate

    h0 = jnp.zeros(states.shape[1:], states.dtype)
    _, prev = lax.scan(step, h0, (states, chunk_decay))
    y_off = jnp.einsum('bclgn,cbgrpn,bgrcl->bclgrp', Cm, prev, jnp.exp(dA_cs))
    y = y_diag + y_off + xs * D_skip.reshape(G, R)[:, :, None]
    y = y.reshape(b, s, G, R * P) * jax.nn.silu(z).reshape(b, s, G, R * P)
    y = rms_norm(y, norm_w.reshape(G, R * P))
    return (y.reshape(b, s, SSM_D_INNER) @ w_out).astype(h.dtype)


def setup_inputs(seed: int = 0) -> dict:
    key = jax.random.key(seed)
    ks = jax.random.split(key, 24)
    f32 = jnp.float32
    nA, nS = N_ATTN_LAYERS, N_SSM_LAYERS
    x = jax.random.normal(ks[0], (BATCH, SEQ, D_MODEL), f32)
    c = jax.random.normal(ks[1], (BATCH, D_MODEL), f32)
    offset = jax.random.randint(ks[2], (BATCH, 1), 0, SEQ, dtype=jnp.int32)
    positions = (jnp.arange(SEQ, dtype=jnp.int32)[None, :] + offset).astype(jnp.int32)
    norm_w = 1.0 + 0.02 * jax.random.normal(ks[3], (DEPTH, D_MODEL), f32)
    ada_w = 0.5 * D_MODEL ** -0.5 * jax.random.normal(ks[4], (DEPTH, D_MODEL, 3 * D_MODEL), f32)
    ada_b = 0.02 * jax.random.normal(ks[5], (DEPTH, 3 * D_MODEL), f32)
    attn_w_in = D_MODEL ** -0.5 * jax.random.normal(ks[6], (nA, D_MODEL, ATTN_IN), f32)
    attn_q_norm = 1.0 + 0.02 * jax.random.normal(ks[7], (nA, ATTN_HEAD_DIM), f32)
    attn_k_norm = 1.0 + 0.02 * jax.random.normal(ks[8], (nA, ATTN_HEAD_DIM), f32)
    attn_lambda_q1 = 0.1 * jax.random.normal(ks[9], (nA, ATTN_HEAD_DIM), f32)
    attn_lambda_k1 = 0.1 * jax.random.normal(ks[10], (nA, ATTN_HEAD_DIM), f32)
    attn_lambda_q2 = 0.1 * jax.random.normal(ks[11], (nA, ATTN_HEAD_DIM), f32)
    attn_lambda_k2 = 0.1 * jax.random.normal(ks[12], (nA, ATTN_HEAD_DIM), f32)
    attn_subln_w = 1.0 + 0.02 * jax.random.normal(ks[13], (nA, ATTN_V_DIM), f32)
    attn_w_out = ATTN_WIDTH ** -0.5 * jax.random.normal(ks[14], (nA, ATTN_WIDTH, D_MODEL), f32)
    ssm_w_in = D_MODEL ** -0.5 * jax.random.normal(ks[15], (nS, D_MODEL, SSM_IN), f32)
    ssm_conv_w = SSM_CONV ** -0.5 * jax.random.normal(ks[16], (nS, SSM_CONV, SSM_CONV_DIM), f32)
    ssm_conv_b = 0.02 * jax.random.normal(ks[17], (nS, SSM_CONV_DIM), f32)
    u = jax.random.uniform(ks[18], (nS, SSM_HEADS), f32)
    dt0 = jnp.exp(u * (math.log(0.1) - math.log(0.001)) + math.log(0.001))
    ssm_dt_bias = dt0 + jnp.log(-jnp.expm1(-dt0))
    ssm_A_log = jnp.log(jax.random.uniform(ks[19], (nS, SSM_HEADS), f32, 1.0, 16.0))
    ssm_D = 1.0 + 0.02 * jax.random.normal(ks[20], (nS, SSM_HEADS), f32)
    ssm_norm_w = 1.0 + 0.02 * jax.random.normal(ks[21], (nS, SSM_D_INNER), f32)
    ssm_w_out = SSM_D_INNER ** -0.5 * jax.random.normal(ks[22], (nS, SSM_D_INNER, D_MODEL), f32)
    return {"x": x, "c": c, "positions": positions, "norm_w": norm_w,
            "ada_w": ada_w, "ada_b": ada_b,
            "attn_w_in": attn_w_in, "attn_q_norm": attn_q_norm, "attn_k_norm": attn_k_norm,
            "attn_lambda_q1": attn_lambda_q1, "attn_lambda_k1": attn_lambda_k1,
            "attn_lambda_q2": attn_lambda_q2, "attn_lambda_k2": attn_lambda_k2,
            "attn_subln_w": attn_subln_w, "attn_w_out": attn_w_out,
            "ssm_w_in": ssm_w_in, "ssm_conv_w": ssm_conv_w, "ssm_conv_b": ssm_conv_b,
            "ssm_dt_bias": ssm_dt_bias, "ssm_A_log": ssm_A_log, "ssm_D": ssm_D,
            "ssm_norm_w": ssm_norm_w, "ssm_w_out": ssm_w_out}


def reference(x, c, positions, norm_w, ada_w, ada_b,
              attn_w_in, attn_q_norm, attn_k_norm,
              attn_lambda_q1, attn_lambda_k1, attn_lambda_q2, attn_lambda_k2,
              attn_subln_w, attn_w_out,
              ssm_w_in, ssm_conv_w, ssm_conv_b, ssm_dt_bias, ssm_A_log, ssm_D,
              ssm_norm_w, ssm_w_out):
    cond = jax.nn.silu(c)
    for layer in range(DEPTH):
        mod = cond @ ada_w[layer] + ada_b[layer]
        shift, scale, gate = jnp.split(mod[:, None, :], 3, axis=-1)
        h = rms_norm(x, norm_w[layer]) * (1.0 + scale) + shift
        j = layer // N_MIXERS
        if layer % N_MIXERS == 0:
            lambda_init = 0.8 - 0.6 * math.exp(-0.3 * layer)
            out = diff_attention(h, positions, attn_w_in[j], attn_q_norm[j], attn_k_norm[j],
                                 attn_lambda_q1[j], attn_lambda_k1[j],
                                 attn_lambda_q2[j], attn_lambda_k2[j],
                                 attn_subln_w[j], attn_w_out[j], lambda_init)
        else:
            out = ssd_mixer(h, ssm_w_in[j], ssm_conv_w[j], ssm_conv_b[j], ssm_dt_bias[j],
                            ssm_A_log[j], ssm_D[j], ssm_norm_w[j], ssm_w_out[j])
        x = x + gate * out
    return x
```

```python
import math
import os
import numpy as np
import ml_dtypes
import concourse.bass as bass
import concourse.mybir as mybir
from concourse.bass_utils import run_bass_kernel_spmd

F32 = mybir.dt.float32
BF16 = mybir.dt.bfloat16
I32 = mybir.dt.int32
AF = mybir.ActivationFunctionType
ALU = mybir.AluOpType
AX = mybir.AxisListType

D = 2048
B = 2
SEQ = 4096
DEPTH = 4
EPS = 1e-6
NCORE = 8
TOK = 1024
PI = math.pi


class Buf:
    __slots__ = ("name", "w", "r")

    def __init__(self, name=""):
        self.name = name
        self.w = None
        self.r = {}


class Sched:
    def __init__(self, nc, n_dma=24):
        self.nc = nc
        self.engs = {"pe": nc.tensor, "act": nc.scalar, "dve": nc.vector,
                     "pool": nc.gpsimd, "sp": nc.sync}
        self.sem = {k: nc.alloc_semaphore("s_" + k) for k in ("pe", "act", "dve", "pool")}
        self.cnt = {k: 0 for k in self.sem}
        self.dsem = [nc.alloc_semaphore("d%d" % i) for i in range(n_dma)]
        self.dcnt = [0] * n_dma
        self.dnext = 0
        self.waited = {}
        self.nbuf = 0

    def buf(self, name=""):
        return Buf(name)

    def bufs(self, n, name=""):
        return [Buf(name + str(i)) for i in range(n)]

    def _wait(self, e, key, val):
        if self.waited.get((e, key), 0) >= val:
            return
        sem = self.sem[key] if isinstance(key, str) else self.dsem[key]
        self.engs[e].wait_ge(sem, val)
        self.waited[(e, key)] = val

    def _deps(self, e, reads, writes, skip_same=False):
        deps = {}
        for b in reads:
            if b.w is not None:
                k, v = b.w
                if v > deps.get(k, 0):
                    deps[k] = v
        for b in writes:
            if b.w is not None:
                k, v = b.w
                if v > deps.get(k, 0):
                    deps[k] = v
            for k, v in b.r.items():
                if v > deps.get(k, 0):
                    deps[k] = v
        for k, v in deps.items():
            if skip_same and k == e:
                continue
            self._wait(e, k, v)

    def op(self, e, fn, reads=(), writes=()):
        self._deps(e, reads, writes, skip_same=(e == "pe"))
        ins = fn()
        self.cnt[e] += 1
        v = self.cnt[e]
        ins.then_inc(self.sem[e], 1)
        for b in writes:
            b.w = (e, v)
            b.r = {}
        for b in reads:
            if b.w is None or b.w != (e, v):
                b.r[e] = v

    def dma(self, q, out, in_, reads=(), writes=(), **kw):
        self._deps(q, reads, writes)
        i = self.dnext
        self.dnext = (self.dnext + 1) % len(self.dsem)
        if self.dcnt[i] > 0:
            self._wait(q, i, self.dcnt[i])
        ins = self.engs[q].dma_start(out=out, in_=in_, **kw)
        self.dcnt[i] += 16
        ins.then_inc(self.dsem[i], 16)
        v = self.dcnt[i]
        for b in writes:
            b.w = (i, v)
            b.r = {}
        for b in reads:
            b.r[i] = v

    def finish(self):
        for i, v in enumerate(self.dcnt):
            if v > 0:
                self._wait("sp", i, v)


def _new_nc():
    return bass.Bass("TRN2", target_bir_lowering=False)


def _run(nc, in_maps):
    res = run_bass_kernel_spmd(nc, in_maps, core_ids=list(range(NCORE)))
    return res.results


def build_mod():
    nc = _new_nc()
    S = Sched(nc)
    cT = nc.dram_tensor("cT", [128, 16, 2], F32, kind="ExternalInput").ap()
    adaw = nc.dram_tensor("adaw", [2048, 3072], F32, kind="ExternalInput").ap()
    adab = nc.dram_tensor("adab", [128, 24], F32, kind="ExternalInput").ap()
    out = nc.dram_tensor("modT", [128, 24, 2], F32, kind="ExternalOutput").ap()
    c_sb = nc.alloc_sbuf_tensor("c_sb", [128, 16, 2], F32)
    cond = nc.alloc_sbuf_tensor("cond", [128, 16, 2], F32)
    bias = nc.alloc_sbuf_tensor("bias", [128, 24], F32)
    res = nc.alloc_sbuf_tensor("res", [128, 24, 2], F32)
    wt = [nc.alloc_sbuf_tensor("wt%d" % i, [128, 16, 512], F32) for i in range(2)]
    ps = nc.alloc_psum_tensor("ps", [128, 24, 2], F32)
    b_c, b_cond, b_bias, b_res, b_ps = S.bufs(5, "m")
    b_w = S.bufs(2, "w")
    S.dma("sp", c_sb[:], cT, writes=[b_c])
    S.dma("sp", bias[:], adab, writes=[b_bias])
    S.op("act", lambda: nc.scalar.activation(out=cond[:], in_=c_sb[:], func=AF.Silu),
         reads=[b_c], writes=[b_cond])
    wv = adaw.rearrange("(kc p) n -> p kc n", p=128)
    for ng in range(6):
        sl = ng % 2
        S.dma("sp", wt[sl][:], wv[:, :, ng * 512:(ng + 1) * 512], writes=[b_w[sl]])
        for nl in range(4):
            n = ng * 4 + nl
            for kc in range(16):
                S.op("pe", lambda: nc.tensor.matmul(
                    ps[:, n, :], lhsT=wt[sl][:, kc, nl * 128:(nl + 1) * 128],
                    rhs=cond[:, kc, :], start=(kc == 0), stop=(kc == 15)),
                    reads=[b_w[sl], b_cond], writes=[b_ps])
    S.op("dve", lambda: nc.vector.tensor_tensor(
        out=res[:], in0=ps[:], in1=bias[:].unsqueeze(2).to_broadcast([128, 24, 2]), op=ALU.add),
        reads=[b_ps, b_bias], writes=[b_res])
    S.dma("sp", out, res[:], reads=[b_res])
    S.finish()
    return nc


def run_mod(c, ada_w, ada_b):
    nc = build_mod()
    cT = np.ascontiguousarray(c.reshape(B, 16, 128).transpose(2, 1, 0))
    in_maps = []
    for i in range(NCORE):
        l, half = i // 2, i % 2
        in_maps.append({
            "cT": cT,
            "adaw": np.ascontiguousarray(ada_w[l][:, half * 3072:(half + 1) * 3072]),
            "adab": np.ascontiguousarray(ada_b[l][half * 3072:(half + 1) * 3072].reshape(24, 128).T),
        })
    res = _run(nc, in_maps)
    mod = np.zeros((DEPTH, B, 3 * D), np.float32)
    for i in range(NCORE):
        l, half = i // 2, i % 2
        o = res[i]["modT"]
        mod[l][:, half * 3072:(half + 1) * 3072] = o.transpose(2, 1, 0).reshape(B, 3072)
    return mod


def modv_layout(mod_lb):
    return np.ascontiguousarray(mod_lb.reshape(3, 16, 128).transpose(2, 0, 1))


def emit_norm(nc, S, x_sb, b_x, modv, b_modv, normw, b_normw, hT_out, ones, b_ones, ps_ss, b_pss, epst, b_eps):
    a = nc.alloc_sbuf_tensor("n_a", [128, 16], F32)
    b_a = S.buf()
    S.op("dve", lambda: nc.vector.scalar_tensor_tensor(
        out=a[:], in0=modv[:, 1, :], scalar=1.0, in1=normw[:], op0=ALU.add, op1=ALU.mult),
        reads=[b_modv, b_normw], writes=[b_a])
    sq = [nc.alloc_sbuf_tensor("n_sq%d" % i, [128, 512], F32) for i in range(2)]
    b_sq = S.bufs(2)
    rs = nc.alloc_sbuf_tensor("n_rs", [128, 512], F32)
    rstd = nc.alloc_sbuf_tensor("n_rstd", [128, 512], F32)
    b_rs, b_rstd = S.bufs(2)
    tmp = [nc.alloc_sbuf_tensor("n_tmp%d" % i, [128, 512], F32) for i in range(2)]
    b_tmp = S.bufs(2)
    ho = [nc.alloc_sbuf_tensor("n_ho%d" % i, [128, 16, 512], BF16) for i in range(2)]
    b_ho = S.bufs(2)
    for tb in range(TOK // 512):
        ts = slice(tb * 512, (tb + 1) * 512)
        for dc in range(16):
            k = dc % 2
            S.op("act", lambda: nc.scalar.activation(out=sq[k][:], in_=x_sb[:, dc, ts], func=AF.Square),
                 reads=[b_x], writes=[b_sq[k]])
            S.op("pe", lambda: nc.tensor.matmul(ps_ss[:], lhsT=ones[:], rhs=sq[k][:],
                                                start=(dc == 0), stop=(dc == 15)),
                 reads=[b_sq[k], b_ones], writes=[b_pss])
        S.op("act", lambda: nc.scalar.activation(out=rs[:], in_=ps_ss[:], func=AF.Sqrt,
                                                 bias=epst[:], scale=1.0 / D),
             reads=[b_pss, b_eps], writes=[b_rs])
        S.op("dve", lambda: nc.vector.reciprocal(out=rstd[:], in_=rs[:]), reads=[b_rs], writes=[b_rstd])
        hb = tb % 2
        for dc in range(16):
            k = dc % 2
            S.op("dve", lambda: nc.vector.tensor_tensor(out=tmp[k][:], in0=x_sb[:, dc, ts], in1=rstd[:],
                                                        op=ALU.mult),
                 reads=[b_x, b_rstd], writes=[b_tmp[k]])
            S.op("act", lambda: nc.scalar.activation(out=ho[hb][:, dc, :], in_=tmp[k][:], func=AF.Identity,
                                                     bias=modv[:, 0, dc:dc + 1], scale=a[:, dc:dc + 1]),
                 reads=[b_tmp[k], b_a, b_modv], writes=[b_ho[hb]])
        S.dma("sp", hT_out.rearrange("c p t -> p c t")[:, :, ts], ho[hb][:], reads=[b_ho[hb]])


def build_norm():
    nc = _new_nc()
    S = Sched(nc)
    xT = nc.dram_tensor("xT", [16, 128, TOK], F32, kind="ExternalInput").ap()
    modv_d = nc.dram_tensor("modv", [128, 3, 16], F32, kind="ExternalInput").ap()
    normw_d = nc.dram_tensor("normw", [128, 16], F32, kind="ExternalInput").ap()
    hT = nc.dram_tensor("hT", [16, 128, TOK], BF16, kind="ExternalOutput").ap()
    x_sb = nc.alloc_sbuf_tensor("x_sb", [128, 16, TOK], F32)
    modv = nc.alloc_sbuf_tensor("modv_sb", [128, 3, 16], F32)
    normw = nc.alloc_sbuf_tensor("normw_sb", [128, 16], F32)
    ones = nc.alloc_sbuf_tensor("ones", [128, 128], F32)
    ps_ss = nc.alloc_psum_tensor("ps_ss", [128, 512], F32)
    b_x, b_modv, b_normw, b_ones, b_pss = S.bufs(5)
    epst = nc.alloc_sbuf_tensor("epst", [128, 1], F32)
    b_eps = S.buf()
    S.op("pool", lambda: nc.gpsimd.memset(epst[:], EPS), writes=[b_eps])
    S.dma("sp", modv[:], modv_d, writes=[b_modv])
    S.dma("sp", normw[:], normw_d, writes=[b_normw])
    S.op("pool", lambda: nc.gpsimd.memset(ones[:], 1.0), writes=[b_ones])
    xv = xT.rearrange("c p t -> p c t")
    b_xs = S.bufs(4)
    for i in range(4):
        S.dma("sp", x_sb[:, 4 * i:4 * i + 4, :], xv[:, 4 * i:4 * i + 4, :], writes=[b_xs[i]])
    for e in ("act", "dve"):
        S._deps(e, b_xs, [])
    emit_norm(nc, S, x_sb, b_x, modv, b_modv, normw, b_normw, hT, ones, b_ones, ps_ss, b_pss, epst, b_eps)
    S.finish()
    return nc


def to_featmajor(x):
    outs = []
    for b in range(B):
        for q in range(4):
            blk = x[b, q * TOK:(q + 1) * TOK, :]
            outs.append(np.ascontiguousarray(blk.T.reshape(16, 128, TOK)))
    return outs


def from_featmajor(parts):
    x = np.zeros((B, SEQ, D), np.float32)
    for b in range(B):
        for q in range(4):
            x[b, q * TOK:(q + 1) * TOK, :] = parts[b * 4 + q].reshape(D, TOK).T
    return x


def build_outproj(WC):
    nc = _new_nc()
    S = Sched(nc)
    yT = nc.dram_tensor("yT", [WC, 128, TOK], BF16, kind="ExternalInput").ap()
    wout = nc.dram_tensor("wout", [WC * 128, D], F32, kind="ExternalInput").ap()
    xT = nc.dram_tensor("xT", [16, 128, TOK], F32, kind="ExternalInput").ap()
    modv_d = nc.dram_tensor("modv", [128, 3, 16], F32, kind="ExternalInput").ap()
    xo = nc.dram_tensor("xo", [16, 128, TOK], F32, kind="ExternalOutput").ap()
    y_sb = nc.alloc_sbuf_tensor("y_sb", [128, WC, TOK], BF16)
    x_sb = nc.alloc_sbuf_tensor("x_sb", [128, 16, TOK], F32)
    modv = nc.alloc_sbuf_tensor("modv_sb", [128, 3, 16], F32)
    wg = [nc.alloc_sbuf_tensor("wg%d" % i, [128, WC, 256], BF16) for i in range(2)]
    ps = [nc.alloc_psum_tensor("ps%d" % i, [128, 512], F32) for i in range(4)]
    b_ps = S.bufs(4)
    b_wg = S.bufs(2)
    b_modv = S.buf()
    b_y = S.bufs(4)
    b_x = S.bufs(16)
    S.dma("sp", modv[:], modv_d, writes=[b_modv])
    yv = yT.rearrange("c p t -> p c t")
    xv = xT.rearrange("c p t -> p c t")
    q = WC // 4
    for i in range(4):
        S.dma("sp", y_sb[:, q * i:q * (i + 1), :], yv[:, q * i:q * (i + 1), :], writes=[b_y[i]])
    for i in range(4):
        S.dma("sp", x_sb[:, 4 * i:4 * i + 4, :], xv[:, 4 * i:4 * i + 4, :], writes=b_x[4 * i:4 * i + 4])
    wv = wout.rearrange("(wc p) n -> p wc n", p=128)
    xov = xo.rearrange("c p t -> p c t")
    k = 0
    for g in range(8):
        sl = g % 2
        S.dma("pool", wg[sl][:], wv[:, :, g * 256:(g + 1) * 256], writes=[b_wg[sl]])
        for dcl in range(2):
            dc = g * 2 + dcl
            for tb in range(TOK // 512):
                ts = slice(tb * 512, (tb + 1) * 512)
                pb = k % 4
                k += 1
                for wc in range(WC):
                    S.op("pe", lambda: nc.tensor.matmul(
                        ps[pb][:], lhsT=wg[sl][:, wc, dcl * 128:(dcl + 1) * 128], rhs=y_sb[:, wc, ts],
                        start=(wc == 0), stop=(wc == WC - 1)),
                        reads=[b_wg[sl], b_y[wc // q]], writes=[b_ps[pb]])
                S.op("dve", lambda: nc.vector.scalar_tensor_tensor(
                    out=x_sb[:, dc, ts], in0=ps[pb][:], scalar=modv[:, 2, dc:dc + 1], in1=x_sb[:, dc, ts],
                    op0=ALU.mult, op1=ALU.add),
                    reads=[b_ps[pb], b_modv], writes=[b_x[dc]])
            S.dma("sp", xov[:, dc, :], x_sb[:, dc, :], reads=[b_x[dc]])
    S.finish()
    return nc


def attn_consts():
    r0t = np.zeros((128, 128), np.float32)
    for pp in range(128):
        if pp % 64 < 32:
            r0t[pp + 32, pp] = -1.0
        else:
            r0t[pp - 32, pp] = 1.0
    bones = np.zeros((128, 128), np.float32)
    bones[:64, :64] = 1.0
    bones[64:, 64:] = 1.0
    k = np.arange(128)
    masktri = (k[:, None] <= k[None, :]).astype(np.float32)
    ident = np.eye(128, dtype=np.float32)
    return np.ascontiguousarray(np.stack([r0t, bones, masktri, ident], axis=1))


def build_attn(lambda_init, stage=9):
    nc = _new_nc()
    S = Sched(nc)
    NH = 4
    hT = nc.dram_tensor("hT", [16, 128, SEQ], BF16, kind="ExternalInput").ap()
    w = nc.dram_tensor("w", [NH, D, 512], F32, kind="ExternalInput").ap()
    pos_d = nc.dram_tensor("pos", [128, SEQ], I32, kind="ExternalInput").ap()
    cst_d = nc.dram_tensor("cst", [128, 4], F32, kind="ExternalInput").ap()
    lamv_d = nc.dram_tensor("lamv", [128, 4, 64], F32, kind="ExternalInput").ap()
    subln_d = nc.dram_tensor("subln", [128, 128], F32, kind="ExternalInput").ap()
    mats_d = nc.dram_tensor("mats", [128, 4, 128], F32, kind="ExternalInput").ap()
    yT = nc.dram_tensor("yT", [NH, 128, SEQ], BF16, kind="ExternalOutput").ap()

    def sb(name, shape, dt):
        return nc.alloc_sbuf_tensor("s_" + name, shape, dt)

    cst = sb("cst", [128, 4], F32)
    lamv = sb("lamv", [128, 4, 64], F32)
    subln = sb("subln", [128, 128], F32)
    mats = sb("mats", [128, 4, 128], BF16)
    cosT = sb("cosT", [128, SEQ], F32)
    sinT = sb("sinT", [128, SEQ], F32)
    cvals = sb("cvals", [128, 4], F32)
    lam = sb("lam", [128, 4], F32)
    Wb = [sb("Wb0", [128, 16, 512], BF16)] * 2
    hblk = [sb("hblk%d" % i, [128, 16, 512], BF16) for i in range(2)]
    QT = sb("QT", [128, SEQ], BF16)
    KT = sb("KT", [128, SEQ], BF16)
    Vaug = sb("Vaug", [128, 32, 132], BF16)
    Gs = sb("Gs", [128, 32, 128], F32)
    yT_sb = [sb("yT_sb0", [128, SEQ], BF16)] * 2
    f32t = [sb("f32t%d" % i, [128, 512], F32) for i in range(8)]
    bft = [sb("bft%d" % i, [128, 512], BF16) for i in range(4)]
    pTs = [sb("pTs%d" % i, [128, 512], BF16) for i in range(3)]
    sm = sb("sm", [128, 16], F32)
    ot = [sb("ot%d" % i, [128, 128], F32) for i in range(2)]
    o2t = [sb("o2t%d" % i, [128, 128], F32) for i in range(2)]
    sqt = [sb("sqt%d" % i, [128, 128], F32) for i in range(2)]
    ybf = [bft[i][:, 0:128] for i in range(4)]

    pA = nc.alloc_psum_tensor("pA", [128, 512], F32)
    pB = nc.alloc_psum_tensor("pB", [128, 512], F32)
    pC = nc.alloc_psum_tensor("pC", [128, 512], F32)
    pD = nc.alloc_psum_tensor("pD", [128, 512], F32)
    pO = [nc.alloc_psum_tensor("pO%d" % i, [128, 512], F32) for i in range(4)]
    pT = pC

    b_cst, b_lamv, b_subln, b_mats, b_cos, b_sin, b_cv, b_lam = S.bufs(8)
    b_W = [S.buf()] * 2
    b_h = S.bufs(2)
    b_QT, b_KT, b_V, b_Vones, b_G = S.bufs(5)
    b_yT = [S.buf()] * 2
    b_f = S.bufs(8)
    b_bf = S.bufs(4)
    b_pTs = S.bufs(3)
    b_pA, b_pB, b_pC, b_pD = S.bufs(4)
    b_pO = S.bufs(4)
    o1s = sb("o1s", [128, 4, 128], F32)
    b_o1s = S.bufs(4)
    b_pT = [b_pC, b_pC]
    b_sm = S.bufs(16)
    b_ot = S.bufs(2)
    b_o2 = S.bufs(2)
    b_sq = S.bufs(2)
    b_yb = b_bf

    def acc(m, qt):
        return pO[qt][:, 0:129]

    S.dma("sp", cst[:], cst_d, writes=[b_cst])
    S.dma("sp", lamv[:], lamv_d, writes=[b_lamv])
    S.dma("sp", subln[:], subln_d, writes=[b_subln])
    S.dma("pool", mats[:], mats_d, writes=[b_mats])
    r0t = mats[:, 0, :]
    bones = mats[:, 1, :]
    masktri = mats[:, 2, :]
    ident = mats[:, 3, :]
    S.op("dve", lambda: nc.vector.memset(cvals[:, 0:1], EPS), writes=[b_cv])
    S.op("dve", lambda: nc.vector.memset(cvals[:, 1:2], -PI), writes=[b_cv])
    S.op("pool", lambda: nc.gpsimd.memset(Vaug[:, :, 128:132], 1.0), writes=[b_Vones])

    posi_t = sb("posi", [128, 1024], I32)
    class _V:
        def __init__(self, ap):
            self.ap = ap

        def __getitem__(self, k):
            return self.ap
    ang = _V(Gs[:, 0:8, :].rearrange("p a b -> p (a b)"))
    arg = _V(Gs[:, 8:16, :].rearrange("p a b -> p (a b)"))
    kf = _V(Gs[:, 16:24, :].rearrange("p a b -> p (a b)"))
    msk = _V(Gs[:, 24:32, :].rearrange("p a b -> p (a b)"))
    b_posi, b_ang, b_arg, b_kf, b_msk = S.bufs(5)
    b_ang = b_arg = b_kf = b_msk = b_G
    C1 = 6.28125
    C2 = 2 * PI - C1
    for c4 in range(4):
        cs_ = slice(c4 * 1024, (c4 + 1) * 1024)
        S.dma("sp", posi_t[:], pos_d[:, cs_], writes=[b_posi])
        S.op("dve", lambda: nc.vector.tensor_copy(out=ang[:], in_=posi_t[:]), reads=[b_posi], writes=[b_ang])
        S.op("dve", lambda: nc.vector.tensor_scalar(out=ang[:], in0=ang[:], scalar1=cst[:, 0:1], scalar2=None,
                                                    op0=ALU.mult), reads=[b_cst], writes=[b_ang])
        S.op("dve", lambda: nc.vector.tensor_scalar(out=posi_t[:], in0=ang[:], scalar1=1.0 / (2 * PI), scalar2=None,
                                                    op0=ALU.mult), reads=[b_ang], writes=[b_posi])
        S.op("dve", lambda: nc.vector.tensor_copy(out=kf[:], in_=posi_t[:]), reads=[b_posi], writes=[b_kf])
        S.op("dve", lambda: nc.vector.scalar_tensor_tensor(out=ang[:], in0=kf[:], scalar=-C1, in1=ang[:],
                                                           op0=ALU.mult, op1=ALU.add), reads=[b_kf], writes=[b_ang])
        S.op("dve", lambda: nc.vector.scalar_tensor_tensor(out=ang[:], in0=kf[:], scalar=-C2, in1=ang[:],
                                                           op0=ALU.mult, op1=ALU.add), reads=[b_kf], writes=[b_ang])
        for which, (shift, dstT, b_dst) in enumerate([(0.0, sinT, b_sin), (0.5 * PI, cosT, b_cos)]):
            S.op("dve", lambda: nc.vector.tensor_scalar(out=msk[:], in0=ang[:], scalar1=shift, scalar2=PI,
                                                        op0=ALU.add, op1=ALU.is_gt), reads=[b_ang], writes=[b_msk])
            S.op("dve", lambda: nc.vector.scalar_tensor_tensor(out=arg[:], in0=msk[:], scalar=-2 * PI, in1=ang[:],
                                                               op0=ALU.mult, op1=ALU.add),
                 reads=[b_msk, b_ang], writes=[b_arg])
            S.op("dve", lambda: nc.vector.tensor_scalar(out=arg[:], in0=arg[:], scalar1=shift, scalar2=PI,
                                                        op0=ALU.add, op1=ALU.min), reads=[], writes=[b_arg])
            S.op("dve", lambda: nc.vector.tensor_scalar(out=arg[:], in0=arg[:], scalar1=-PI, scalar2=None,
                                                        op0=ALU.max), reads=[], writes=[b_arg])
            S.op("act", lambda: nc.scalar.activation(out=dstT[:, cs_], in_=arg[:], func=AF.Sin),
                 reads=[b_arg], writes=[b_dst])

    S.op("dve", lambda: nc.vector.tensor_tensor(out=ot[0][:, 0:64], in0=lamv[:, 0, :], in1=lamv[:, 1, :], op=ALU.mult),
         reads=[b_lamv], writes=[b_ot[0]])
    S.op("dve", lambda: nc.vector.tensor_tensor(out=ot[0][:, 64:128], in0=lamv[:, 2, :], in1=lamv[:, 3, :], op=ALU.mult),
         reads=[b_lamv], writes=[b_ot[0]])
    S.op("dve", lambda: nc.vector.tensor_reduce(out=lam[:, 0:1], in_=ot[0][:, 0:64], axis=AX.X, op=ALU.add),
         reads=[b_ot[0]], writes=[b_lam])
    S.op("dve", lambda: nc.vector.tensor_reduce(out=lam[:, 1:2], in_=ot[0][:, 64:128], axis=AX.X, op=ALU.add),
         reads=[b_ot[0]], writes=[b_lam])
    S.op("act", lambda: nc.scalar.activation(out=lam[:, 0:2], in_=lam[:, 0:2], func=AF.Exp), reads=[b_lam], writes=[b_lam])
    S.op("dve", lambda: nc.vector.tensor_tensor(out=lam[:, 2:3], in0=lam[:, 0:1], in1=lam[:, 1:2], op=ALU.subtract),
         reads=[b_lam], writes=[b_lam])
    S.op("dve", lambda: nc.vector.tensor_scalar(out=lam[:, 3:4], in0=lam[:, 2:3], scalar1=float(lambda_init), scalar2=None,
                                                op0=ALU.add), reads=[b_lam], writes=[b_lam])
    lam_ap = lam[:, 3:4]
    S.op("dve", lambda: nc.vector.tensor_scalar(out=cvals[:, 2:3], in0=lam[:, 3:4], scalar1=-1.0, scalar2=None,
                                                op0=ALU.mult), reads=[b_lam], writes=[b_cv])
    neglam_ap = cvals[:, 2:3]
    S.op("dve", lambda: nc.vector.tensor_scalar(out=subln[:], in0=subln[:], scalar1=float(1.0 - lambda_init),
                                                scalar2=None, op0=ALU.mult), reads=[b_subln], writes=[b_subln])

    hv = hT.rearrange("c p t -> p c t")
    eps_ap = cvals[:, 0:1]
    pending = []
    hcount = 0
    for j in range(NH if stage >= 1 else 0):
        ws = j % 2
        S.dma("pool", Wb[ws][:], w[j].rearrange("(kc p) n -> p kc n", p=128), writes=[b_W[ws]])
        W = Wb[ws]
        for tb in range(8):
            hs = hcount % 2
            hcount += 1
            ts = slice(tb * 512, (tb + 1) * 512)
            S.dma("sp", hblk[hs][:], hv[:, :, ts], writes=[b_h[hs]])
            hb = hblk[hs]
            for which in range(2):
                c0 = which * 128
                nw = cst[:, 1 + which:2 + which]
                dst, b_dst = (QT, b_QT) if which == 0 else (KT, b_KT)
                pacc, b_pacc = (pA, b_pA) if which == 0 else (pB, b_pB)
                for kc in range(16):
                    S.op("pe", lambda: nc.tensor.matmul(pacc[:], lhsT=W[:, kc, c0:c0 + 128], rhs=hb[:, kc, :],
                                                        start=(kc == 0), stop=(kc == 15)),
                         reads=[b_W[ws], b_h[hs]], writes=[b_pacc])
                qw, b_qw = bft[which * 2], b_bf[which * 2]
                sq, b_sqq = bft[which * 2 + 1], b_bf[which * 2 + 1]
                S.op("act", lambda: nc.scalar.activation(out=qw[:], in_=pacc[:], func=AF.Identity, scale=nw),
                     reads=[b_pacc, b_cst], writes=[b_qw])
                S.op("act", lambda: nc.scalar.activation(out=sq[:], in_=pacc[:], func=AF.Square),
                     reads=[b_pacc], writes=[b_sqq])
                S.op("pe", lambda: nc.tensor.matmul(pC[:], lhsT=bones, rhs=sq[:], start=True, stop=True),
                     reads=[b_sqq, b_mats], writes=[b_pC])
                S.op("pe", lambda: nc.tensor.matmul(pD[:], lhsT=r0t, rhs=qw[:], start=True, stop=True),
                     reads=[b_qw, b_mats], writes=[b_pD])
                f0, f1, f2, f3 = [f32t[which * 4 + i] for i in range(4)]
                g0, g1, g2, g3 = [b_f[which * 4 + i] for i in range(4)]
                S.op("act", lambda: nc.scalar.activation(out=f0[:], in_=pC[:], func=AF.Sqrt, bias=eps_ap, scale=1.0 / 64),
                     reads=[b_pC, b_cv], writes=[g0])
                S.op("dve", lambda: nc.vector.reciprocal(out=f1[:], in_=f0[:]), reads=[g0], writes=[g1])
                S.op("dve", lambda: nc.vector.scalar_tensor_tensor(out=f2[:], in0=pacc[:], scalar=nw, in1=cosT[:, ts],
                                                                   op0=ALU.mult, op1=ALU.mult),
                     reads=[b_pacc, b_cst, b_cos, b_qw, b_sqq], writes=[g2])
                S.op("dve", lambda: nc.vector.tensor_tensor(out=f3[:], in0=pD[:], in1=sinT[:, ts], op=ALU.mult),
                     reads=[b_pD, b_sin], writes=[g3])
                S.op("pool", lambda: nc.gpsimd.tensor_tensor(out=f2[:], in0=f2[:], in1=f3[:], op=ALU.add),
                     reads=[g3], writes=[g2])
                S.op("pool", lambda: nc.gpsimd.tensor_tensor(out=dst[:, ts], in0=f2[:], in1=f1[:], op=ALU.mult),
                     reads=[g2, g1], writes=[b_dst])
            for tt in range(4):
                tile = tb * 4 + tt
                pacc, b_pacc = (pA, b_pA) if tt % 2 == 0 else (pB, b_pB)
                for kc in range(16):
                    S.op("pe", lambda: nc.tensor.matmul(pacc[:, 0:256], lhsT=hb[:, kc, tt * 128:(tt + 1) * 128],
                                                        rhs=W[:, kc, 256:512], start=(kc == 0), stop=(kc == 15)),
                         reads=[b_W[ws], b_h[hs]], writes=[b_pacc])
                S.op("act", lambda: nc.scalar.activation(out=Vaug[:, tile, 0:128], in_=pacc[:, 0:128], func=AF.Copy),
                     reads=[b_pacc], writes=[b_V])
                gt, b_gt = ot[tt % 2], b_ot[tt % 2]
                S.op("act", lambda: nc.scalar.activation(out=gt[:], in_=pacc[:, 128:256], func=AF.Silu),
                     reads=[b_pacc], writes=[b_gt])
                S.op("dve", lambda: nc.vector.tensor_tensor(out=Gs[:, tile, :], in0=gt[:], in1=subln[:], op=ALU.mult),
                     reads=[b_gt, b_subln], writes=[b_G])
        ys = j % 2
        ycur = yT_sb[ys]
        scount = 0
        for qb in range(8 if stage >= 3 else (1 if stage == 2 else 0)):
            for m in range(2):
                ps_ = slice(m * 64, (m + 1) * 64)
                for kt in range(4 * qb + 4):
                    r = kt - 4 * qb
                    q0 = max(r, 0) * 128
                    n = 512 - q0
                    pS, b_pS = (pA, b_pA) if scount % 2 == 0 else (pB, b_pB)
                    pt, b_pt = pTs[scount % 3], b_pTs[scount % 3]
                    scount += 1
                    S.op("pe", lambda: nc.tensor.matmul(pS[:, 0:n], lhsT=KT[ps_, kt * 128:(kt + 1) * 128],
                                                        rhs=QT[ps_, qb * 512 + q0:qb * 512 + 512], start=True, stop=True),
                         reads=[b_KT, b_QT], writes=[b_pS])
                    S.op("act", lambda: nc.scalar.activation(out=pt[:, 0:n], in_=pS[:, 0:n], func=AF.Exp, scale=0.125),
                         reads=[b_pS], writes=[b_pt])
                    if r >= 0:
                        S.op("pool", lambda: nc.gpsimd.tensor_tensor(out=pt[:, 0:128], in0=pt[:, 0:128], in1=masktri,
                                                                     op=ALU.mult),
                             reads=[b_mats], writes=[b_pt])
                    for qt in range(max(r, 0), 4 if int(os.environ.get('SUB', '9')) >= 1 else 0):
                        S.op("pe", lambda: nc.tensor.matmul(acc(m, qt), lhsT=pt[:, qt * 128 - q0:qt * 128 - q0 + 128],
                                                            rhs=Vaug[:, kt, 0:129], start=(kt == 0), stop=(kt == 4 * qb + qt)),
                             reads=[b_pt, b_V, b_Vones], writes=[b_pO[qt]])
                if m == 0:
                    for qt in range(4):
                        e0 = qt % 2
                        a1 = acc(0, qt)
                        S.op("dve", lambda: nc.vector.reciprocal(out=sm[:, 12 + qt:13 + qt], in_=a1[:, 128:129]),
                             reads=[b_pO[qt]], writes=[b_sm[2 + qt]])
                        S.op("dve", lambda: nc.vector.tensor_scalar(out=o1s[:, qt, :], in0=a1[:, 0:128],
                                                                    scalar1=sm[:, 12 + qt:13 + qt], scalar2=None, op0=ALU.mult),
                             reads=[b_pO[qt], b_sm[2 + qt]], writes=[b_o1s[qt]])
            for fn in pending:
                fn()
            pending = []
            for qt in range(4 if int(os.environ.get('SUB', '9')) >= 2 else 0):
                tile = qb * 4 + qt
                e = tile % 2
                a2 = acc(1, qt)
                sm_ = sm[:, e * 6:e * 6 + 6]
                bs = b_sm[e]
                S.op("dve", lambda: nc.vector.reciprocal(out=sm_[:, 1:2], in_=a2[:, 128:129]), reads=[b_pO[qt]], writes=[bs])
                S.op("dve", lambda: nc.vector.tensor_tensor(out=sm_[:, 2:3], in0=sm_[:, 1:2], in1=neglam_ap, op=ALU.mult),
                     reads=[b_cv], writes=[bs])
                S.op("dve", lambda: nc.vector.scalar_tensor_tensor(out=ot[e][:], in0=a2[:, 0:128], scalar=sm_[:, 2:3],
                                                                   in1=o1s[:, qt, :], op0=ALU.mult, op1=ALU.add),
                     reads=[b_pO[qt], bs, b_o1s[qt]], writes=[b_ot[e]])
                S.op("pool", lambda: nc.gpsimd.tensor_tensor(out=sqt[e][:], in0=ot[e][:], in1=ot[e][:], op=ALU.mult),
                     reads=[b_ot[e]], writes=[b_sq[e]])
                S.op("dve", lambda: nc.vector.tensor_reduce(out=sm_[:, 3:4], in_=sqt[e][:], axis=AX.X, op=ALU.add),
                     reads=[b_sq[e]], writes=[bs])
                S.op("act", lambda: nc.scalar.activation(out=sm_[:, 4:5], in_=sm_[:, 3:4], func=AF.Sqrt, bias=eps_ap,
                                                         scale=1.0 / 128), reads=[bs, b_cv], writes=[bs])
                S.op("dve", lambda: nc.vector.reciprocal(out=sm_[:, 5:6], in_=sm_[:, 4:5]), reads=[bs], writes=[bs])
                S.op("dve", lambda: nc.vector.scalar_tensor_tensor(out=ybf[qt], in0=ot[e][:], scalar=sm_[:, 5:6],
                                                                   in1=Gs[:, tile, :], op0=ALU.mult, op1=ALU.mult),
                     reads=[b_ot[e], bs, b_G], writes=[b_yb[qt]])

                def tr(e=e, tile=tile, ycur=ycur, ys=ys, qt=qt):
                    if os.environ.get('NOPE') is None:
                        S.op("pe", lambda: nc.tensor.matmul(pT[:, e * 128:(e + 1) * 128], lhsT=ybf[qt], rhs=ident,
                                                            start=True, stop=True),
                             reads=[b_yb[qt], b_mats], writes=[b_pT[e]])
                    S.op("dve", lambda: nc.vector.tensor_copy(out=ycur[:, tile * 128:(tile + 1) * 128],
                                                              in_=pT[:, e * 128:(e + 1) * 128]),
                         reads=[b_pT[e]], writes=[b_yT[ys]])
                if int(os.environ.get('SUB', '9')) >= 3:
                    pending.append(tr)
        for fn in pending:
            fn()
        pending = []
        S.dma("sp", yT[j], ycur[:], reads=[b_yT[ys]])
    S.finish()
    return nc


def attn_inputs(hT_b, pos_b, w_in, qn, kn, lq1, lk1, lq2, lk2, subln, hg):
    heads = range(hg * 4, hg * 4 + 4)
    ws = []
    for hd in heads:
        cols = []
        for blk in range(4):
            cols.append(w_in[:, blk * 2048 + hd * 128: blk * 2048 + (hd + 1) * 128])
        ws.append(np.concatenate(cols, axis=1))
    w = np.ascontiguousarray(np.stack(ws, axis=0))
    inv_freq = (10000.0 ** (-np.arange(0, 64, 2, dtype=np.float32) / np.float32(64))).astype(np.float32)
    p = np.arange(128)
    cst = np.zeros((128, 4), np.float32)
    cst[:, 0] = inv_freq[p % 32]
    cst[:, 1] = qn[p % 64]
    cst[:, 2] = kn[p % 64]
    lamv = np.ascontiguousarray(np.broadcast_to(np.stack([lq1, lk1, lq2, lk2], 0)[None], (128, 4, 64))).astype(np.float32)
    return {
        "hT": hT_b, "w": w,
        "pos": np.ascontiguousarray(np.broadcast_to(pos_b[None, :], (128, SEQ))).astype(np.int32),
        "cst": cst, "lamv": lamv,
        "subln": np.ascontiguousarray(np.broadcast_to(subln[None, :], (128, 128))).astype(np.float32),
        "mats": attn_consts(),
    }


def ssd_consts():
    k = np.arange(128)
    tri = (k[:, None] <= k[None, :]).astype(np.float32)
    ones = np.ones((128, 128), np.float32)
    ident = np.eye(128, dtype=np.float32)
    mats = np.ascontiguousarray(np.stack([tri, ones, ident], axis=1))
    diag = np.zeros((128, 8, 128), np.float32)
    for h in range(8):
        diag[h, h, :] = 1.0
    return mats, diag


def build_ssd():
    nc = _new_nc()
    S = Sched(nc)
    hT = nc.dram_tensor("hT", [16, 128, SEQ], BF16, kind="ExternalInput").ap()
    w = nc.dram_tensor("w", [2, D, 1288], F32, kind="ExternalInput").ap()
    convw_d = nc.dram_tensor("convw", [2, 128, 6, 4], F32, kind="ExternalInput").ap()
    convb_d = nc.dram_tensor("convb", [2, 128, 6], F32, kind="ExternalInput").ap()
    hp_d = nc.dram_tensor("hp", [2, 128, 24], F32, kind="ExternalInput").ap()
    normw_d = nc.dram_tensor("normw", [2, 128, 512], F32, kind="ExternalInput").ap()
    mats_d = nc.dram_tensor("mats", [128, 3, 128], F32, kind="ExternalInput").ap()
    diag_d = nc.dram_tensor("diag", [128, 8, 128], F32, kind="ExternalInput").ap()
    yT = nc.dram_tensor("yT", [8, 128, SEQ], BF16, kind="ExternalOutput").ap()

    def sb(name, shape, dt):
        return nc.alloc_sbuf_tensor("s_" + name, shape, dt)

    matsf = sb("matsf", [128, 3, 128], F32)
    matsb = sb("matsb", [128, 3, 128], BF16)
    diag = sb("diag", [128, 8, 128], F32)
    Wb = sb("Wb", [128, 16, 1536], BF16)
    hblk = [sb("hblk%d" % i, [128, 16, 512], BF16) for i in range(2)]
    convw = sb("convw", [128, 6, 4], F32)
    convb = sb("convb", [128, 6], F32)
    hp = sb("hp", [128, 24], F32)
    normw = sb("normw", [128, 512], F32)
    Dvec = sb("Dvec", [128, 8, 64], F32)
    Aneg = sb("Aneg", [128, 8], F32)
    cvals = sb("cvals", [128, 4], F32)
    pre = sb("pre", [128, 6, 516], F32)
    cacc = [sb("cacc%d" % i, [128, 512], F32) for i in range(2)]
    xsT = [sb("xsT%d" % i, [128, 512], BF16) for i in range(2)]
    BT = sb("BT", [128, 512], BF16)
    CT = sb("CT", [128, 512], BF16)
    xs_tok = sb("xs_tok", [128, 4, 512], BF16)
    B_tok = sb("B_tok", [128, 4, 128], BF16)
    zs = sb("zs", [128, 512], F32)
    sm = sb("sm", [128, 128], F32)
    csT_sb = sb("csT_sb", [128, 128], F32)
    rhsD = sb("rhsD", [128, 3, 8, 128], BF16)
    ab = sb("ab", [128, 512], BF16)
    b_ab = S.buf()
    segt = sb("segt", [128, 8, 128], F32)
    ecsb = sb("ecsb", [128, 8, 128], F32)
    CBm = sb("CBm", [128, 128], F32)
    MT = sb("MT", [128, 8, 128], BF16)
    CsT = sb("CsT", [128, 8, 128], BF16)
    xdt = sb("xdt", [128, 8, 128], BF16)
    xds = sb("xds", [128, 8, 64], BF16)
    prevT = sb("prevT", [128, 8, 64], F32)
    prevb = sb("prevb", [128, 8, 128], BF16)
    tst = sb("tst", [128, 8, 64], F32)
    t1 = sb("t1", [128, 512], F32)
    t2 = sb("t2", [128, 512], F32)
    t3 = sb("t3", [128, 512], F32)
    ynb = sb("ynb", [128, 512], BF16)
    yT_sb = sb("yT_sb", [128, 4, 512], BF16)

    pA = nc.alloc_psum_tensor("pA", [128, 512], F32)
    pB = nc.alloc_psum_tensor("pB", [128, 512], F32)
    pSm = nc.alloc_psum_tensor("pSm", [128, 512], F32)
    pCs = [nc.alloc_psum_tensor("pCs%d" % i, [128, 512], F32) for i in range(2)]
    pY = nc.alloc_psum_tensor("pY", [128, 512], F32)
    pSt = nc.alloc_psum_tensor("pSt", [128, 512], F32)
    pT = nc.alloc_psum_tensor("pT", [128, 512], F32)

    (b_mf, b_mb, b_diag, b_W, b_cw, b_cb, b_hp, b_nw, b_Dv, b_An, b_cv, b_BT, b_CT, b_xs, b_Bt, b_zs,
     b_csT, b_rhsD, b_seg, b_ecs, b_CBm, b_MT, b_CsT, b_xdt, b_xds, b_prev, b_prevb, b_tst, b_t1, b_t2, b_t3,
     b_ynb, b_yT, b_pA, b_pB, b_pY, b_pSt, b_pT) = S.bufs(38)
    b_h = S.bufs(2)
    b_pre = S.bufs(6)
    b_cacc = S.bufs(2)
    b_xsT = S.bufs(2)
    b_pCs = S.bufs(2)
    b_sm = S.bufs(16)
    b_pSm = S.bufs(5)

    S.dma("sp", matsf[:], mats_d, writes=[b_mf])
    S.dma("pool", matsb[:], mats_d, writes=[b_mb])
    S.dma("sp", diag[:], diag_d, writes=[b_diag])
    tri_f, ones_f = matsf[:, 0, :], matsf[:, 1, :]
    tri_b, ones_b = matsb[:, 0, :], matsb[:, 1, :]
    ident_b = matsb[:, 2, :]
    S.op("dve", lambda: nc.vector.memset(cvals[:, 0:1], EPS), writes=[b_cv])
    S.op("dve", lambda: nc.vector.memset(cvals[:, 1:2], 1.0), writes=[b_cv])
    S.op("pool", lambda: nc.gpsimd.memset(xdt[:], 0.0), writes=[b_xdt])
    hv = hT.rearrange("c p t -> p c t")
    hcount = 0

    def SM(i):
        return sm[:, i * 8:(i + 1) * 8]

    for gi in range(2):
        S.dma("pool", Wb[:, :, 0:1288], w[gi].rearrange("(kc p) n -> p kc n", p=128), writes=[b_W])
        S.dma("sp", convw[:], convw_d[gi], writes=[b_cw])
        S.dma("sp", convb[:], convb_d[gi], writes=[b_cb])
        S.dma("sp", hp[:], hp_d[gi], writes=[b_hp])
        S.dma("sp", normw[:], normw_d[gi], writes=[b_nw])
        S.op("act", lambda: nc.scalar.activation(out=Aneg[:], in_=hp[:, 8:16], func=AF.Exp), reads=[b_hp], writes=[b_An])
        S.op("dve", lambda: nc.vector.tensor_scalar(out=Aneg[:], in0=Aneg[:], scalar1=-1.0, scalar2=None, op0=ALU.mult),
             writes=[b_An])
        S.op("dve", lambda: nc.vector.tensor_copy(out=Dvec[:], in_=hp[:, 16:24].unsqueeze(2).to_broadcast([128, 8, 64])),
             reads=[b_hp], writes=[b_Dv])
        S.op("pool", lambda: nc.gpsimd.memset(pre[:, :, 0:3], 0.0), writes=b_pre)
        S.op("pool", lambda: nc.gpsimd.memset(prevT[:], 0.0), writes=[b_prev])
        S.op("pool", lambda: nc.gpsimd.memset(prevb[:], 0.0), writes=[b_prevb])
        for tb in range(int(os.environ.get('NTB', '8'))):
            hs = hcount % 2
            hcount += 1
            ts = slice(tb * 512, (tb + 1) * 512)
            S.dma("sp", hblk[hs][:], hv[:, :, ts], writes=[b_h[hs]])
            hb = hblk[hs]
            for cc in range(6):
                pacc, b_pacc = (pA, b_pA) if cc % 2 == 0 else (pB, b_pB)
                for kc in range(16):
                    S.op("pe", lambda: nc.tensor.matmul(pacc[:], lhsT=Wb[:, kc, cc * 128:(cc + 1) * 128], rhs=hb[:, kc, :],
                                                        start=(kc == 0), stop=(kc == 15)),
                         reads=[b_W, b_h[hs]], writes=[b_pacc])
                S.op("act", lambda: nc.scalar.activation(out=pre[:, cc, 3:515], in_=pacc[:], func=AF.Copy),
                     reads=[b_pacc], writes=[b_pre[cc]])
                ca, b_ca = cacc[cc % 2], b_cacc[cc % 2]
                S.op("dve", lambda: nc.vector.tensor_scalar(out=ca[:], in0=pre[:, cc, 0:512], scalar1=convw[:, cc, 0:1],
                                                            scalar2=None, op0=ALU.mult),
                     reads=[b_pre[cc], b_cw], writes=[b_ca])
                for k in range(1, 4):
                    S.op("dve", lambda: nc.vector.scalar_tensor_tensor(out=ca[:], in0=pre[:, cc, k:k + 512],
                                                                       scalar=convw[:, cc, k:k + 1], in1=ca[:],
                                                                       op0=ALU.mult, op1=ALU.add),
                         reads=[b_pre[cc], b_cw], writes=[b_ca])
                if cc < 4:
                    dst, b_dst = xsT[cc % 2], b_xsT[cc % 2]
                elif cc == 4:
                    dst, b_dst = BT, b_BT
                else:
                    dst, b_dst = CT, b_CT
                S.op("act", lambda: nc.scalar.activation(out=dst[:], in_=ca[:], func=AF.Silu, bias=convb[:, cc:cc + 1]),
                     reads=[b_ca, b_cb], writes=[b_dst])
                S.op("pool", lambda: nc.gpsimd.tensor_copy(out=pre[:, cc, 0:3], in_=pre[:, cc, 512:515]),
                     writes=[b_pre[cc]])
                if cc < 5 and os.environ.get('NOTR') is None:
                    for tt in range(4):
                        S.op("pe", lambda: nc.tensor.matmul(pT[:, tt * 128:(tt + 1) * 128], lhsT=dst[:, tt * 128:(tt + 1) * 128],
                                                            rhs=ident_b, start=True, stop=True),
                             reads=[b_dst, b_mb], writes=[b_pT])
                    if cc < 4:
                        for tt in range(4):
                            S.op("dve", lambda: nc.vector.tensor_copy(out=xs_tok[:, tt, cc * 128:(cc + 1) * 128],
                                                                      in_=pT[:, tt * 128:(tt + 1) * 128]),
                                 reads=[b_pT], writes=[b_xs])
                    else:
                        S.op("dve", lambda: nc.vector.tensor_copy(out=B_tok[:].rearrange("p a b -> p (a b)"), in_=pT[:]),
                             reads=[b_pT], writes=[b_Bt])
            for c in range(4):
                if int(os.environ.get('SST', '9')) < 1:
                    continue
                cs_ = slice(c * 128, (c + 1) * 128)
                for kc in range(16):
                    S.op("pe", lambda: nc.tensor.matmul(pA[:], lhsT=hb[:, kc, cs_], rhs=Wb[:, kc, 768:1280],
                                                        start=(kc == 0), stop=(kc == 15)),
                         reads=[b_W, b_h[hs]], writes=[b_pA])
                for kc in range(16):
                    S.op("pe", lambda: nc.tensor.matmul(pSm[:, 0:8], lhsT=hb[:, kc, cs_], rhs=Wb[:, kc, 1280:1288],
                                                        start=(kc == 0), stop=(kc == 15)),
                         reads=[b_W, b_h[hs]], writes=[b_pSm[0]])
                S.op("act", lambda: nc.scalar.activation(out=zs[:], in_=pA[:], func=AF.Silu), reads=[b_pA], writes=[b_zs])
                if int(os.environ.get('SSUB', '9')) < 1:
                    continue
                S.op("dve", lambda: nc.vector.tensor_tensor(out=SM(0), in0=pSm[:, 0:8], in1=hp[:, 0:8], op=ALU.add),
                     reads=[b_pSm[0], b_hp], writes=[b_sm[0]])
                S.op("dve", lambda: nc.vector.tensor_scalar(out=SM(11), in0=SM(0), scalar1=-1.0, scalar2=None, op0=ALU.mult),
                     reads=[b_sm[0]], writes=[b_sm[14]])
                S.op("dve", lambda: nc.vector.tensor_tensor(out=SM(1), in0=SM(0), in1=SM(11), op=ALU.max),
                     reads=[b_sm[0], b_sm[14]], writes=[b_sm[1]])
                S.op("act", lambda: nc.scalar.activation(out=SM(2), in_=SM(1), func=AF.Exp, scale=-1.0),
                     reads=[b_sm[1]], writes=[b_sm[2]])
                S.op("act", lambda: nc.scalar.activation(out=SM(3), in_=SM(2), func=AF.Ln, bias=cvals[:, 1:2]),
                     reads=[b_sm[2], b_cv], writes=[b_sm[3]])
                S.op("dve", lambda: nc.vector.tensor_scalar(out=SM(4), in0=SM(0), scalar1=0.0, scalar2=None, op0=ALU.max),
                     reads=[b_sm[0]], writes=[b_sm[4]])
                S.op("dve", lambda: nc.vector.tensor_tensor(out=SM(5), in0=SM(4), in1=SM(3), op=ALU.add),
                     reads=[b_sm[4], b_sm[3]], writes=[b_sm[5]])
                S.op("dve", lambda: nc.vector.tensor_tensor(out=SM(6), in0=SM(5), in1=Aneg[:], op=ALU.mult),
                     reads=[b_sm[5], b_An], writes=[b_sm[6]])
                if int(os.environ.get('SSUB', '9')) < 2:
                    continue
                S.op("dve", lambda: nc.vector.tensor_copy(out=ab[:, 0:8], in_=SM(6)), reads=[b_sm[6]], writes=[b_ab])
                S.op("dve", lambda: nc.vector.tensor_tensor(out=SM(12), in0=SM(6), in1=ab[:, 0:8], op=ALU.subtract),
                     reads=[b_sm[6], b_ab], writes=[b_sm[15]])
                S.op("dve", lambda: nc.vector.tensor_copy(out=ab[:, 8:16], in_=SM(12)), reads=[b_sm[15]], writes=[b_ab])
                S.op("dve", lambda: nc.vector.tensor_tensor(out=SM(12), in0=SM(12), in1=ab[:, 8:16], op=ALU.subtract),
                     reads=[b_ab], writes=[b_sm[15]])
                S.op("dve", lambda: nc.vector.tensor_copy(out=ab[:, 16:24], in_=SM(12)), reads=[b_sm[15]], writes=[b_ab])
                for j3 in range(3):
                    S.op("pe", lambda: nc.tensor.matmul(pSm[:, 8:16], lhsT=tri_b, rhs=ab[:, j3 * 8:(j3 + 1) * 8],
                                                        start=(j3 == 0), stop=(j3 == 2)),
                         reads=[b_ab, b_mb], writes=[b_pSm[1]])
                for j3 in range(3):
                    S.op("pe", lambda: nc.tensor.matmul(pSm[:, 16:24], lhsT=ones_b, rhs=ab[:, j3 * 8:(j3 + 1) * 8],
                                                        start=(j3 == 0), stop=(j3 == 2)),
                         reads=[b_ab, b_mb], writes=[b_pSm[2]])
                if int(os.environ.get('SSUB', '9')) < 3:
                    continue
                S.op("dve", lambda: nc.vector.tensor_copy(out=SM(7), in_=pSm[:, 8:16]),
                     reads=[b_pSm[1]], writes=[b_sm[7]])
                S.op("dve", lambda: nc.vector.tensor_tensor(out=SM(8), in0=pSm[:, 16:24], in1=SM(7), op=ALU.subtract),
                     reads=[b_pSm[2], b_sm[7]], writes=[b_sm[8]])
                S.op("act", lambda: nc.scalar.activation(out=SM(9), in_=SM(8), func=AF.Exp), reads=[b_sm[8]], writes=[b_sm[9]])
                S.op("dve", lambda: nc.vector.tensor_tensor(out=SM(10), in0=SM(5), in1=SM(9), op=ALU.mult),
                     reads=[b_sm[5], b_sm[9]], writes=[b_sm[10]])
                if int(os.environ.get('SST', '9')) < 2:
                    continue
                for j3 in range(3):
                    S.op("dve", lambda: nc.vector.tensor_tensor(
                        out=rhsD[:, j3, :, :], in0=tri_b.unsqueeze(1).to_broadcast([128, 8, 128]),
                        in1=ab[:, j3 * 8:(j3 + 1) * 8].unsqueeze(2).to_broadcast([128, 8, 128]), op=ALU.mult),
                         reads=[b_ab, b_mb], writes=[b_rhsD])
                if int(os.environ.get('S2', '9')) < 1:
                    continue
                for half in range(2):
                    for j3 in range(3):
                        S.op("pe", lambda: nc.tensor.matmul(
                            pCs[half][:], lhsT=ones_b,
                            rhs=rhsD[:, j3, half * 4:(half + 1) * 4, :].rearrange("p a b -> p (a b)"),
                            start=(j3 == 0), stop=(j3 == 2)),
                             reads=[b_rhsD, b_mb], writes=[b_pCs[half]])
                if int(os.environ.get('S2', '9')) < 2:
                    continue
                for half in range(2):
                    hsl = slice(half * 4, (half + 1) * 4)
                    for h4 in range(0 if os.environ.get('NOSEG') else 4):
                        hh = half * 4 + h4
                        S.op("dve", lambda: nc.vector.tensor_scalar(out=segt[:, hh, :], in0=pCs[half][:, h4 * 128:(h4 + 1) * 128],
                                                                    scalar1=sm[:, 56 + hh:57 + hh], scalar2=0.0,
                                                                    op0=ALU.subtract, op1=ALU.min),
                             reads=[b_pCs[half], b_sm[7]], writes=[b_seg])
                    if os.environ.get('NOECS') is None:
                        S.op("act", lambda: nc.scalar.activation(out=ecsb[:, hsl, :].rearrange("p a b -> p (a b)"),
                                                                 in_=pCs[half][:], func=AF.Exp),
                             reads=[b_pCs[half], b_seg], writes=[b_ecs])
                if int(os.environ.get('S2', '9')) < 3:
                    continue
                S.op("act", lambda: nc.scalar.activation(out=segt[:].rearrange("p a b -> p (a b)"),
                                                         in_=segt[:].rearrange("p a b -> p (a b)"), func=AF.Exp), writes=[b_seg])
                if int(os.environ.get('SST', '9')) < 3:
                    continue
                S.op("pe", lambda: nc.tensor.matmul(pSm[:, 256:384], lhsT=BT[:, cs_], rhs=CT[:, cs_], start=True, stop=True),
                     reads=[b_BT, b_CT], writes=[b_pSm[4]])
                S.op("dve", lambda: nc.vector.tensor_tensor(out=CBm[:], in0=pSm[:, 256:384], in1=tri_f, op=ALU.mult),
                     reads=[b_pSm[4], b_mf], writes=[b_CBm])
                S.op("pool", lambda: nc.gpsimd.tensor_tensor(out=MT[:], in0=segt[:],
                                                             in1=CBm[:].unsqueeze(1).to_broadcast([128, 8, 128]), op=ALU.mult),
                     reads=[b_seg, b_CBm], writes=[b_MT])
                S.op("dve", lambda: nc.vector.tensor_tensor(out=CsT[:], in0=ecsb[:],
                                                            in1=CT[:, cs_].unsqueeze(1).to_broadcast([128, 8, 128]), op=ALU.mult),
                     reads=[b_ecs, b_CT], writes=[b_CsT])
                xs3 = xs_tok[:, c, :].rearrange("p (a b) -> p a b", a=8)
                S.op("dve", lambda: nc.vector.tensor_tensor(out=xdt[:, :, 0:64], in0=xs3,
                                                            in1=SM(5).unsqueeze(2).to_broadcast([128, 8, 64]), op=ALU.mult),
                     reads=[b_xs, b_sm[5]], writes=[b_xdt])
                S.op("pool", lambda: nc.gpsimd.tensor_tensor(out=xds[:], in0=xs3,
                                                             in1=SM(10).unsqueeze(2).to_broadcast([128, 8, 64]), op=ALU.mult),
                     reads=[b_xs, b_sm[10]], writes=[b_xds])
                for h in range(8):
                    S.op("pe", lambda: nc.tensor.matmul(pY[:, h * 64:(h + 1) * 64], lhsT=MT[:, h, :], rhs=xdt[:, h, 0:64],
                                                        start=True, stop=False),
                         reads=[b_MT, b_xdt], writes=[b_pY])
                    S.op("pe", lambda: nc.tensor.matmul(pY[:, h * 64:(h + 1) * 64], lhsT=CsT[:, h, :], rhs=prevb[:, h, 0:64],
                                                        start=False, stop=True),
                         reads=[b_CsT, b_prevb], writes=[b_pY])
                S.op("pe", lambda: nc.tensor.matmul(pSt[:], lhsT=B_tok[:, c, :], rhs=xds[:].rearrange("p a b -> p (a b)"),
                                                    start=True, stop=True),
                     reads=[b_Bt, b_xds], writes=[b_pSt])
                if int(os.environ.get('SST', '9')) < 4:
                    continue
                S.op("dve", lambda: nc.vector.tensor_tensor(out=tst[:], in0=prevT[:],
                                                            in1=ecsb[:, :, 127:128].to_broadcast([128, 8, 64]), op=ALU.mult),
                     reads=[b_prev, b_ecs], writes=[b_tst])
                S.op("dve", lambda: nc.vector.tensor_tensor(out=prevT[:], in0=tst[:],
                                                            in1=pSt[:].rearrange("p (a b) -> p a b", a=8), op=ALU.add),
                     reads=[b_tst, b_pSt], writes=[b_prev])
                S.op("pool", lambda: nc.gpsimd.tensor_copy(out=prevb[:, :, 0:64], in_=prevT[:]), reads=[b_prev], writes=[b_prevb])
                if int(os.environ.get('SST', '9')) < 5:
                    continue
                S.op("pool", lambda: nc.gpsimd.tensor_tensor(out=t1[:], in0=xs_tok[:, c, :], in1=Dvec[:].rearrange("p a b -> p (a b)"),
                                                             op=ALU.mult), reads=[b_xs, b_Dv], writes=[b_t1])
                S.op("dve", lambda: nc.vector.tensor_tensor(out=t2[:], in0=pY[:], in1=t1[:], op=ALU.add),
                     reads=[b_pY, b_t1], writes=[b_t2])
                S.op("pool", lambda: nc.gpsimd.tensor_tensor(out=t2[:], in0=t2[:], in1=zs[:], op=ALU.mult),
                     reads=[b_zs], writes=[b_t2])
                S.op("pool", lambda: nc.gpsimd.tensor_tensor(out=t3[:], in0=t2[:], in1=t2[:], op=ALU.mult),
                     reads=[b_t2], writes=[b_t3])
                S.op("dve", lambda: nc.vector.tensor_reduce(out=sm[:, 100:101], in_=t3[:], axis=AX.X, op=ALU.add),
                     reads=[b_t3], writes=[b_sm[11]])
                S.op("act", lambda: nc.scalar.activation(out=sm[:, 101:102], in_=sm[:, 100:101], func=AF.Sqrt, bias=cvals[:, 0:1],
                                                         scale=1.0 / 512), reads=[b_sm[11], b_cv], writes=[b_sm[12]])
                S.op("dve", lambda: nc.vector.reciprocal(out=sm[:, 102:103], in_=sm[:, 101:102]), reads=[b_sm[12]], writes=[b_sm[13]])
                S.op("dve", lambda: nc.vector.scalar_tensor_tensor(out=ynb[:], in0=t2[:], scalar=sm[:, 102:103], in1=normw[:],
                                                                   op0=ALU.mult, op1=ALU.mult),
                     reads=[b_t2, b_sm[13], b_nw], writes=[b_ynb])
                for cc in range(4):
                    S.op("pe", lambda: nc.tensor.matmul(pT[:, cc * 128:(cc + 1) * 128], lhsT=ynb[:, cc * 128:(cc + 1) * 128],
                                                        rhs=ident_b, start=True, stop=True),
                         reads=[b_ynb, b_mb], writes=[b_pT])
                for cc in range(4):
                    S.op("dve", lambda: nc.vector.tensor_copy(out=yT_sb[:, cc, cs_], in_=pT[:, cc * 128:(cc + 1) * 128]),
                         reads=[b_pT], writes=[b_yT])
            S.dma("sp", yT[gi * 4:(gi + 1) * 4].rearrange("c p t -> p c t")[:, :, ts], yT_sb[:], reads=[b_yT])
    S.finish()
    return nc


def ssd_inputs(hT_b, w_in, conv_w, conv_b, dt_bias, A_log, Dp, norm_w, hg):
    ws, cws, cbs, hps, nws = [], [], [], [], []
    for gi in range(2):
        g = hg * 2 + gi
        xcols = w_in[:, 4096 + g * 512: 4096 + (g + 1) * 512]
        bcols = w_in[:, 8192 + g * 128: 8192 + (g + 1) * 128]
        ccols = w_in[:, 9216 + g * 128: 9216 + (g + 1) * 128]
        zcols = w_in[:, g * 512:(g + 1) * 512]
        dcols = w_in[:, 10240 + g * 8: 10240 + (g + 1) * 8]
        ws.append(np.concatenate([xcols, bcols, ccols, zcols, dcols], axis=1))
        ch = np.concatenate([np.arange(g * 512, (g + 1) * 512), 4096 + np.arange(g * 128, (g + 1) * 128),
                             5120 + np.arange(g * 128, (g + 1) * 128)])
        cw = conv_w[:, ch]
        cws.append(cw.T.reshape(6, 128, 4).transpose(1, 0, 2))
        cbs.append(conv_b[ch].reshape(6, 128).T)
        hv = np.concatenate([dt_bias[g * 8:(g + 1) * 8], A_log[g * 8:(g + 1) * 8], Dp[g * 8:(g + 1) * 8]])
        hps.append(np.broadcast_to(hv[None, :], (128, 24)))
        nws.append(np.broadcast_to(norm_w[g * 512:(g + 1) * 512][None, :], (128, 512)))
    mats, diag = ssd_consts()
    f = lambda a: np.ascontiguousarray(np.stack(a, 0)).astype(np.float32)
    return {"hT": hT_b, "w": f(ws), "convw": f(cws), "convb": f(cbs), "hp": f(hps), "normw": f(nws),
            "mats": mats, "diag": diag}


def _gather_hT(h_parts):
    out = []
    for b in range(B):
        out.append(np.ascontiguousarray(np.concatenate([h_parts[b * 4 + q] for q in range(4)], axis=2)))
    return out


def _scatter_yT(y_parts, nch):
    outs = []
    for b in range(B):
        full = np.concatenate([y_parts[b * 4 + hg] for hg in range(4)], axis=0)
        for q in range(4):
            outs.append(np.ascontiguousarray(full[:, :, q * TOK:(q + 1) * TOK]))
    return outs


def kernel(x, c, positions, norm_w, ada_w, ada_b,
           attn_w_in, attn_q_norm, attn_k_norm,
           attn_lambda_q1, attn_lambda_k1, attn_lambda_q2, attn_lambda_k2,
           attn_subln_w, attn_w_out,
           ssm_w_in, ssm_conv_w, ssm_conv_b, ssm_dt_bias, ssm_A_log, ssm_D,
           ssm_norm_w, ssm_w_out):
    f = lambda a: np.asarray(a)
    x, c, positions = f(x).astype(np.float32), f(c).astype(np.float32), f(positions).astype(np.int32)
    mod = run_mod(c, f(ada_w), f(ada_b))
    xs = to_featmajor(x)
    for layer in range(DEPTH):
        j = layer // 2
        modv = [modv_layout(mod[layer][i // 4]) for i in range(NCORE)]
        normw = np.ascontiguousarray(f(norm_w)[layer].reshape(16, 128).T)
        res = _run(build_norm(), [{"xT": xs[i], "modv": modv[i], "normw": normw} for i in range(NCORE)])
        hT = _gather_hT([res[i]["hT"] for i in range(NCORE)])
        if layer % 2 == 0:
            li = 0.8 - 0.6 * math.exp(-0.3 * layer)
            in_maps = [attn_inputs(hT[i // 4], positions[i // 4], f(attn_w_in)[j], f(attn_q_norm)[j], f(attn_k_norm)[j],
                                   f(attn_lambda_q1)[j], f(attn_lambda_k1)[j], f(attn_lambda_q2)[j], f(attn_lambda_k2)[j],
                                   f(attn_subln_w)[j], i % 4) for i in range(NCORE)]
            res = _run(build_attn(li), in_maps)
            yparts = _scatter_yT([res[i]["yT"] for i in range(NCORE)], 4)
            wout, WC = f(attn_w_out)[j], 16
        else:
            in_maps = [ssd_inputs(hT[i // 4], f(ssm_w_in)[j], f(ssm_conv_w)[j], f(ssm_conv_b)[j], f(ssm_dt_bias)[j],
                                  f(ssm_A_log)[j], f(ssm_D)[j], f(ssm_norm_w)[j], i % 4) for i in range(NCORE)]
            res = _run(build_ssd(), in_maps)
            yparts = _scatter_yT([res[i]["yT"] for i in range(NCORE)], 8)
            wout, WC = f(ssm_w_out)[j], 32
        wout = np.ascontiguousarray(wout.astype(np.float32))
        res = _run(build_outproj(WC), [{"yT": yparts[i], "wout": wout, "xT": xs[i], "modv": modv[i]} for i in range(NCORE)])
        xs = [res[i]["xo"] for i in range(NCORE)]
    return from_featmajor(xs)
```

```python
import math
import os
import numpy as np
import ml_dtypes
import concourse.bass as bass
import concourse.mybir as mybir
from concourse.bass_utils import run_bass_kernel_spmd

F32 = mybir.dt.float32
BF16 = mybir.dt.bfloat16
I32 = mybir.dt.int32
AF = mybir.ActivationFunctionType
ALU = mybir.AluOpType
AX = mybir.AxisListType

D = 2048
B = 2
SEQ = 4096
DEPTH = 4
EPS = 1e-6
NCORE = 8
TOK = 1024
PI = math.pi


class Buf:
    __slots__ = ("name", "w", "r")

    def __init__(self, name=""):
        self.name = name
        self.w = None
        self.r = {}


class Sched:
    def __init__(self, nc, n_dma=24):
        self.nc = nc
        self.engs = {"pe": nc.tensor, "act": nc.scalar, "dve": nc.vector,
                     "pool": nc.gpsimd, "sp": nc.sync}
        self.sem = {k: nc.alloc_semaphore("s_" + k) for k in ("pe", "act", "dve", "pool")}
        self.cnt = {k: 0 for k in self.sem}
        self.dsem = [nc.alloc_semaphore("d%d" % i) for i in range(n_dma)]
        self.dcnt = [0] * n_dma
        self.dnext = 0
        self.waited = {}
        self.nbuf = 0

    def buf(self, name=""):
        return Buf(name)

    def bufs(self, n, name=""):
        return [Buf(name + str(i)) for i in range(n)]

    def _wait(self, e, key, val):
        if self.waited.get((e, key), 0) >= val:
            return
        sem = self.sem[key] if isinstance(key, str) else self.dsem[key]
        self.engs[e].wait_ge(sem, val)
        self.waited[(e, key)] = val

    def _deps(self, e, reads, writes, skip_same=False):
        deps = {}
        for b in reads:
            if b.w is not None:
                k, v = b.w
                if v > deps.get(k, 0):
                    deps[k] = v
        for b in writes:
            if b.w is not None:
                k, v = b.w
                if v > deps.get(k, 0):
                    deps[k] = v
            for k, v in b.r.items():
                if v > deps.get(k, 0):
                    deps[k] = v
        for k, v in deps.items():
            if skip_same and k == e:
                continue
            self._wait(e, k, v)

    def op(self, e, fn, reads=(), writes=()):
        self._deps(e, reads, writes, skip_same=(e == "pe"))
        ins = fn()
        self.cnt[e] += 1
        v = self.cnt[e]
        ins.then_inc(self.sem[e], 1)
        for b in writes:
            b.w = (e, v)
            b.r = {}
        for b in reads:
            if b.w is None or b.w != (e, v):
                b.r[e] = v

    def dma(self, q, out, in_, reads=(), writes=(), **kw):
        self._deps(q, reads, writes)
        i = self.dnext
        self.dnext = (self.dnext + 1) % len(self.dsem)
        if self.dcnt[i] > 0:
            self._wait(q, i, self.dcnt[i])
        ins = self.engs[q].dma_start(out=out, in_=in_, **kw)
        self.dcnt[i] += 16
        ins.then_inc(self.dsem[i], 16)
        v = self.dcnt[i]
        for b in writes:
            b.w = (i, v)
            b.r = {}
        for b in reads:
            b.r[i] = v

    def finish(self):
        for i, v in enumerate(self.dcnt):
            if v > 0:
                self._wait("sp", i, v)


def _new_nc():
    return bass.Bass("TRN2", target_bir_lowering=False)


def _run(nc, in_maps):
    res = run_bass_kernel_spmd(nc, in_maps, core_ids=list(range(NCORE)))
    return res.results


def build_mod():
    nc = _new_nc()
    S = Sched(nc)
    cT = nc.dram_tensor("cT", [128, 16, 2], F32, kind="ExternalInput").ap()
    adaw = nc.dram_tensor("adaw", [2048, 3072], F32, kind="ExternalInput").ap()
    adab = nc.dram_tensor("adab", [128, 24], F32, kind="ExternalInput").ap()
    out = nc.dram_tensor("modT", [128, 24, 2], F32, kind="ExternalOutput").ap()
    c_sb = nc.alloc_sbuf_tensor("c_sb", [128, 16, 2], F32)
    cond = nc.alloc_sbuf_tensor("cond", [128, 16, 2], F32)
    bias = nc.alloc_sbuf_tensor("bias", [128, 24], F32)
    res = nc.alloc_sbuf_tensor("res", [128, 24, 2], F32)
    wt = [nc.alloc_sbuf_tensor("wt%d" % i, [128, 16, 512], F32) for i in range(2)]
    ps = nc.alloc_psum_tensor("ps", [128, 24, 2], F32)
    b_c, b_cond, b_bias, b_res, b_ps = S.bufs(5, "m")
    b_w = S.bufs(2, "w")
    S.dma("sp", c_sb[:], cT, writes=[b_c])
    S.dma("sp", bias[:], adab, writes=[b_bias])
    S.op("act", lambda: nc.scalar.activation(out=cond[:], in_=c_sb[:], func=AF.Silu),
         reads=[b_c], writes=[b_cond])
    wv = adaw.rearrange("(kc p) n -> p kc n", p=128)
    for ng in range(6):
        sl = ng % 2
        S.dma("sp", wt[sl][:], wv[:, :, ng * 512:(ng + 1) * 512], writes=[b_w[sl]])
        for nl in range(4):
            n = ng * 4 + nl
            for kc in range(16):
                S.op("pe", lambda: nc.tensor.matmul(
                    ps[:, n, :], lhsT=wt[sl][:, kc, nl * 128:(nl + 1) * 128],
                    rhs=cond[:, kc, :], start=(kc == 0), stop=(kc == 15)),
                    reads=[b_w[sl], b_cond], writes=[b_ps])
    S.op("dve", lambda: nc.vector.tensor_tensor(
        out=res[:], in0=ps[:], in1=bias[:].unsqueeze(2).to_broadcast([128, 24, 2]), op=ALU.add),
        reads=[b_ps, b_bias], writes=[b_res])
    S.dma("sp", out, res[:], reads=[b_res])
    S.finish()
    return nc


def run_mod(c, ada_w, ada_b):
    nc = build_mod()
    cT = np.ascontiguousarray(c.reshape(B, 16, 128).transpose(2, 1, 0))
    in_maps = []
    for i in range(NCORE):
        l, half = i // 2, i % 2
        in_maps.append({
            "cT": cT,
            "adaw": np.ascontiguousarray(ada_w[l][:, half * 3072:(half + 1) * 3072]),
            "adab": np.ascontiguousarray(ada_b[l][half * 3072:(half + 1) * 3072].reshape(24, 128).T),
        })
    res = _run(nc, in_maps)
    mod = np.zeros((DEPTH, B, 3 * D), np.float32)
    for i in range(NCORE):
        l, half = i // 2, i % 2
        o = res[i]["modT"]
        mod[l][:, half * 3072:(half + 1) * 3072] = o.transpose(2, 1, 0).reshape(B, 3072)
    return mod


def modv_layout(mod_lb):
    return np.ascontiguousarray(mod_lb.reshape(3, 16, 128).transpose(2, 0, 1))


def emit_norm(nc, S, x_sb, b_x, modv, b_modv, normw, b_normw, hT_out, ones, b_ones, ps_ss, b_pss, epst, b_eps,
              ho=None, b_ho=None):
    a = nc.alloc_sbuf_tensor("n_a", [128, 16], F32)
    b_a = S.buf()
    S.op("dve", lambda: nc.vector.scalar_tensor_tensor(
        out=a[:], in0=modv[:, 1, :], scalar=1.0, in1=normw[:], op0=ALU.add, op1=ALU.mult),
        reads=[b_modv, b_normw], writes=[b_a])
    sq = [nc.alloc_sbuf_tensor("n_sq%d" % i, [128, 512], F32) for i in range(2)]
    b_sq = S.bufs(2)
    rs = nc.alloc_sbuf_tensor("n_rs", [128, 512], F32)
    rstd = nc.alloc_sbuf_tensor("n_rstd", [128, 512], F32)
    b_rs, b_rstd = S.bufs(2)
    tmp = [nc.alloc_sbuf_tensor("n_tmp%d" % i, [128, 512], F32) for i in range(2)]
    b_tmp = S.bufs(2)
    if ho is None:
        ho_t = [nc.alloc_sbuf_tensor("n_ho%d" % i, [128, 16, 512], BF16) for i in range(2)]
        ho = [t[:] for t in ho_t]
        b_ho = S.bufs(2)
    for tb in range(TOK // 512):
        ts = slice(tb * 512, (tb + 1) * 512)
        for dc in range(16):
            k = dc % 2
            S.op("act", lambda: nc.scalar.activation(out=sq[k][:], in_=x_sb[:, dc, ts], func=AF.Square),
                 reads=[b_x], writes=[b_sq[k]])
            S.op("pe", lambda: nc.tensor.matmul(ps_ss[:], lhsT=ones[:], rhs=sq[k][:],
                                                start=(dc == 0), stop=(dc == 15)),
                 reads=[b_sq[k], b_ones], writes=[b_pss])
        S.op("act", lambda: nc.scalar.activation(out=rs[:], in_=ps_ss[:], func=AF.Sqrt,
                                                 bias=epst[:], scale=1.0 / D),
             reads=[b_pss, b_eps], writes=[b_rs])
        S.op("dve", lambda: nc.vector.reciprocal(out=rstd[:], in_=rs[:]), reads=[b_rs], writes=[b_rstd])
        hb = tb % 2
        for dc in range(16):
            k = dc % 2
            S.op("dve", lambda: nc.vector.tensor_tensor(out=tmp[k][:], in0=x_sb[:, dc, ts], in1=rstd[:],
                                                        op=ALU.mult),
                 reads=[b_x, b_rstd], writes=[b_tmp[k]])
            S.op("act", lambda: nc.scalar.activation(out=ho[hb][:, dc, :], in_=tmp[k][:], func=AF.Identity,
                                                     bias=modv[:, 0, dc:dc + 1], scale=a[:, dc:dc + 1]),
                 reads=[b_tmp[k], b_a, b_modv], writes=[b_ho[hb]])
        S.dma("sp", hT_out.rearrange("c p t -> p c t")[:, :, ts], ho[hb], reads=[b_ho[hb]])


def build_norm():
    nc = _new_nc()
    S = Sched(nc)
    xT = nc.dram_tensor("xT", [16, 128, TOK], F32, kind="ExternalInput").ap()
    modv_d = nc.dram_tensor("modv", [128, 3, 16], F32, kind="ExternalInput").ap()
    normw_d = nc.dram_tensor("normw", [128, 16], F32, kind="ExternalInput").ap()
    hT = nc.dram_tensor("hT", [16, 128, TOK], BF16, kind="ExternalOutput").ap()
    x_sb = nc.alloc_sbuf_tensor("x_sb", [128, 16, TOK], F32)
    modv = nc.alloc_sbuf_tensor("modv_sb", [128, 3, 16], F32)
    normw = nc.alloc_sbuf_tensor("normw_sb", [128, 16], F32)
    ones = nc.alloc_sbuf_tensor("ones", [128, 128], F32)
    ps_ss = nc.alloc_psum_tensor("ps_ss", [128, 512], F32)
    b_x, b_modv, b_normw, b_ones, b_pss = S.bufs(5)
    epst = nc.alloc_sbuf_tensor("epst", [128, 1], F32)
    b_eps = S.buf()
    S.op("pool", lambda: nc.gpsimd.memset(epst[:], EPS), writes=[b_eps])
    S.dma("sp", modv[:], modv_d, writes=[b_modv])
    S.dma("sp", normw[:], normw_d, writes=[b_normw])
    S.op("pool", lambda: nc.gpsimd.memset(ones[:], 1.0), writes=[b_ones])
    xv = xT.rearrange("c p t -> p c t")
    b_xs = S.bufs(4)
    for i in range(4):
        S.dma("sp", x_sb[:, 4 * i:4 * i + 4, :], xv[:, 4 * i:4 * i + 4, :], writes=[b_xs[i]])
    for e in ("act", "dve"):
        S._deps(e, b_xs, [])
    emit_norm(nc, S, x_sb, b_x, modv, b_modv, normw, b_normw, hT, ones, b_ones, ps_ss, b_pss, epst, b_eps)
    S.finish()
    return nc


def to_featmajor(x):
    outs = []
    for b in range(B):
        for q in range(4):
            blk = x[b, q * TOK:(q + 1) * TOK, :]
            outs.append(np.ascontiguousarray(blk.T.reshape(16, 128, TOK)))
    return outs


def from_featmajor(parts):
    x = np.zeros((B, SEQ, D), np.float32)
    for b in range(B):
        for q in range(4):
            x[b, q * TOK:(q + 1) * TOK, :] = parts[b * 4 + q].reshape(D, TOK).T
    return x


def build_outproj(WC, with_norm=False):
    nc = _new_nc()
    S = Sched(nc)
    yT = nc.dram_tensor("yT", [WC, 128, TOK], BF16, kind="ExternalInput").ap()
    wout = nc.dram_tensor("wout", [WC * 128, D], F32, kind="ExternalInput").ap()
    xT = nc.dram_tensor("xT", [16, 128, TOK], F32, kind="ExternalInput").ap()
    modv_d = nc.dram_tensor("modv", [128, 3, 16], F32, kind="ExternalInput").ap()
    xo = nc.dram_tensor("xo", [16, 128, TOK], F32, kind="ExternalOutput").ap()
    y_sb = nc.alloc_sbuf_tensor("y_sb", [128, WC, TOK], BF16)
    x_sb = nc.alloc_sbuf_tensor("x_sb", [128, 16, TOK], F32)
    modv = nc.alloc_sbuf_tensor("modv_sb", [128, 3, 16], F32)
    wg = [nc.alloc_sbuf_tensor("wg%d" % i, [128, WC, 256], BF16) for i in range(2)]
    ps = [nc.alloc_psum_tensor("ps%d" % i, [128, 512], F32) for i in range(4)]
    b_ps = S.bufs(4)
    b_wg = S.bufs(2)
    b_modv = S.buf()
    b_y = S.bufs(4)
    b_x = S.bufs(16)
    S.dma("sp", modv[:], modv_d, writes=[b_modv])
    yv = yT.rearrange("c p t -> p c t")
    xv = xT.rearrange("c p t -> p c t")
    q = WC // 4
    for i in range(4):
        S.dma("sp", y_sb[:, q * i:q * (i + 1), :], yv[:, q * i:q * (i + 1), :], writes=[b_y[i]])
    for i in range(4):
        S.dma("sp", x_sb[:, 4 * i:4 * i + 4, :], xv[:, 4 * i:4 * i + 4, :], writes=b_x[4 * i:4 * i + 4])
    wv = wout.rearrange("(wc p) n -> p wc n", p=128)
    xov = xo.rearrange("c p t -> p c t")
    k = 0
    for g in range(8):
        sl = g % 2
        S.dma("pool", wg[sl][:], wv[:, :, g * 256:(g + 1) * 256], writes=[b_wg[sl]])
        for dcl in range(2):
            dc = g * 2 + dcl
            for tb in range(TOK // 512):
                ts = slice(tb * 512, (tb + 1) * 512)
                pb = k % 4
                k += 1
                for wc in range(WC):
                    S.op("pe", lambda: nc.tensor.matmul(
                        ps[pb][:], lhsT=wg[sl][:, wc, dcl * 128:(dcl + 1) * 128], rhs=y_sb[:, wc, ts],
                        start=(wc == 0), stop=(wc == WC - 1)),
                        reads=[b_wg[sl], b_y[wc // q]], writes=[b_ps[pb]])
                S.op("dve", lambda: nc.vector.scalar_tensor_tensor(
                    out=x_sb[:, dc, ts], in0=ps[pb][:], scalar=modv[:, 2, dc:dc + 1], in1=x_sb[:, dc, ts],
                    op0=ALU.mult, op1=ALU.add),
                    reads=[b_ps[pb], b_modv], writes=[b_x[dc]])
            S.dma("sp", xov[:, dc, :], x_sb[:, dc, :], reads=[b_x[dc]])
    if with_norm:
        modv2_d = nc.dram_tensor("modv2", [128, 3, 16], F32, kind="ExternalInput").ap()
        normw_d = nc.dram_tensor("normw", [128, 16], F32, kind="ExternalInput").ap()
        hT = nc.dram_tensor("hT", [16, 128, TOK], BF16, kind="ExternalOutput").ap()
        modv2 = nc.alloc_sbuf_tensor("modv2_sb", [128, 3, 16], F32)
        normw = nc.alloc_sbuf_tensor("normw_sb", [128, 16], F32)
        ones = nc.alloc_sbuf_tensor("ones", [128, 128], F32)
        epst = nc.alloc_sbuf_tensor("epst", [128, 1], F32)
        b_modv2, b_normw, b_ones, b_eps, b_xall = S.bufs(5)
        S.dma("sp", modv2[:], modv2_d, writes=[b_modv2])
        S.dma("sp", normw[:], normw_d, writes=[b_normw])
        S.op("pool", lambda: nc.gpsimd.memset(ones[:], 1.0), writes=[b_ones])
        S.op("pool", lambda: nc.gpsimd.memset(epst[:], EPS), writes=[b_eps])
        for e in ("act", "dve"):
            S._deps(e, b_x, [])
        if WC == 32:
            ho = [y_sb[:, 8 * i:8 * i + 8, :].rearrange("p a (b c) -> p (a b) c", b=2) for i in range(2)]
            b_ho = [b_y[0], b_y[1]]
        else:
            ho, b_ho = None, None
        emit_norm(nc, S, x_sb, b_xall, modv2, b_modv2, normw, b_normw, hT, ones, b_ones, ps[0], b_ps[0], epst, b_eps,
                  ho=ho, b_ho=b_ho)
    S.finish()
    return nc


def attn_consts():
    r0t = np.zeros((128, 128), np.float32)
    for pp in range(128):
        if pp % 64 < 32:
            r0t[pp + 32, pp] = -1.0
        else:
            r0t[pp - 32, pp] = 1.0
    bones = np.zeros((128, 128), np.float32)
    bones[:64, :64] = 1.0
    bones[64:, 64:] = 1.0
    k = np.arange(128)
    masktri = (k[:, None] <= k[None, :]).astype(np.float32)
    ident = np.eye(128, dtype=np.float32)
    return np.ascontiguousarray(np.stack([r0t, bones, masktri, ident], axis=1))


def build_attn(lambda_init, stage=9):
    nc = _new_nc()
    S = Sched(nc)
    NH = 4
    hT = nc.dram_tensor("hT", [16, 128, SEQ], BF16, kind="ExternalInput").ap()
    w = nc.dram_tensor("w", [NH, D, 512], F32, kind="ExternalInput").ap()
    pos_d = nc.dram_tensor("pos", [128, SEQ], I32, kind="ExternalInput").ap()
    cst_d = nc.dram_tensor("cst", [128, 4], F32, kind="ExternalInput").ap()
    lamv_d = nc.dram_tensor("lamv", [128, 4, 64], F32, kind="ExternalInput").ap()
    subln_d = nc.dram_tensor("subln", [128, 128], F32, kind="ExternalInput").ap()
    mats_d = nc.dram_tensor("mats", [128, 4, 128], F32, kind="ExternalInput").ap()
    yT = nc.dram_tensor("yT", [NH, 128, SEQ], BF16, kind="ExternalOutput").ap()

    def sb(name, shape, dt):
        return nc.alloc_sbuf_tensor("s_" + name, shape, dt)

    cst = sb("cst", [128, 4], F32)
    lamv = sb("lamv", [128, 4, 64], F32)
    subln = sb("subln", [128, 128], F32)
    mats = sb("mats", [128, 4, 128], BF16)
    cosT = sb("cosT", [128, SEQ], F32)
    sinT = sb("sinT", [128, SEQ], F32)
    cvals = sb("cvals", [128, 4], F32)
    lam = sb("lam", [128, 4], F32)
    Wb = [sb("Wb0", [128, 16, 512], BF16)] * 2
    hblk = [sb("hblk%d" % i, [128, 16, 512], BF16) for i in range(2)]
    QT = sb("QT", [128, SEQ], BF16)
    KT = sb("KT", [128, SEQ], BF16)
    Vaug = sb("Vaug", [128, 32, 132], BF16)
    Gs = sb("Gs", [128, 32, 128], F32)
    yT_sb = [sb("yT_sb0", [128, SEQ], BF16)] * 2
    f32t = [sb("f32t%d" % i, [128, 512], F32) for i in range(8)]
    bft = [sb("bft%d" % i, [128, 512], BF16) for i in range(4)]
    pTs = [sb("pTs%d" % i, [128, 512], BF16) for i in range(3)]
    sm = sb("sm", [128, 16], F32)
    ot = [sb("ot%d" % i, [128, 128], F32) for i in range(2)]
    o2t = [sb("o2t%d" % i, [128, 128], F32) for i in range(2)]
    sqt = [sb("sqt%d" % i, [128, 128], F32) for i in range(2)]
    ybf = [bft[i][:, 0:128] for i in range(4)]

    pA = nc.alloc_psum_tensor("pA", [128, 512], F32)
    pB = nc.alloc_psum_tensor("pB", [128, 512], F32)
    pC = nc.alloc_psum_tensor("pC", [128, 512], F32)
    pD = nc.alloc_psum_tensor("pD", [128, 512], F32)
    pO = [nc.alloc_psum_tensor("pO%d" % i, [128, 512], F32) for i in range(4)]
    pT = pC

    b_cst, b_lamv, b_subln, b_mats, b_cos, b_sin, b_cv, b_lam = S.bufs(8)
    b_W = [S.buf()] * 2
    b_h = S.bufs(2)
    b_QT, b_KT, b_V, b_Vones, b_G = S.bufs(5)
    b_yT = [S.buf()] * 2
    b_f = S.bufs(8)
    b_bf = S.bufs(4)
    b_pTs = S.bufs(3)
    b_pA, b_pB, b_pC, b_pD = S.bufs(4)
    b_pO = S.bufs(4)
    o1s = sb("o1s", [128, 4, 128], F32)
    b_o1s = S.bufs(4)
    b_pT = [b_pC, b_pC]
    b_sm = S.bufs(16)
    b_ot = S.bufs(2)
    b_o2 = S.bufs(2)
    b_sq = S.bufs(2)
    b_yb = b_bf

    def acc(m, qt):
        return pO[qt][:, 0:129]

    S.dma("sp", cst[:], cst_d, writes=[b_cst])
    S.dma("sp", lamv[:], lamv_d, writes=[b_lamv])
    S.dma("sp", subln[:], subln_d, writes=[b_subln])
    S.dma("pool", mats[:], mats_d, writes=[b_mats])
    r0t = mats[:, 0, :]
    bones = mats[:, 1, :]
    masktri = mats[:, 2, :]
    ident = mats[:, 3, :]
    S.op("dve", lambda: nc.vector.memset(cvals[:, 0:1], EPS), writes=[b_cv])
    S.op("dve", lambda: nc.vector.memset(cvals[:, 1:2], -PI), writes=[b_cv])
    S.op("pool", lambda: nc.gpsimd.memset(Vaug[:, :, 128:132], 1.0), writes=[b_Vones])

    posi_t = sb("posi", [128, 1024], I32)
    class _V:
        def __init__(self, ap):
            self.ap = ap

        def __getitem__(self, k):
            return self.ap
    ang = _V(Gs[:, 0:8, :].rearrange("p a b -> p (a b)"))
    arg = _V(Gs[:, 8:16, :].rearrange("p a b -> p (a b)"))
    kf = _V(Gs[:, 16:24, :].rearrange("p a b -> p (a b)"))
    msk = _V(Gs[:, 24:32, :].rearrange("p a b -> p (a b)"))
    b_posi, b_ang, b_arg, b_kf, b_msk = S.bufs(5)
    b_ang = b_arg = b_kf = b_msk = b_G
    C1 = 6.28125
    C2 = 2 * PI - C1
    for c4 in range(4):
        cs_ = slice(c4 * 1024, (c4 + 1) * 1024)
        S.dma("sp", posi_t[:], pos_d[:, cs_], writes=[b_posi])
        S.op("dve", lambda: nc.vector.tensor_copy(out=ang[:], in_=posi_t[:]), reads=[b_posi], writes=[b_ang])
        S.op("dve", lambda: nc.vector.tensor_scalar(out=ang[:], in0=ang[:], scalar1=cst[:, 0:1], scalar2=None,
                                                    op0=ALU.mult), reads=[b_cst], writes=[b_ang])
        S.op("dve", lambda: nc.vector.tensor_scalar(out=posi_t[:], in0=ang[:], scalar1=1.0 / (2 * PI), scalar2=None,
                                                    op0=ALU.mult), reads=[b_ang], writes=[b_posi])
        S.op("dve", lambda: nc.vector.tensor_copy(out=kf[:], in_=posi_t[:]), reads=[b_posi], writes=[b_kf])
        S.op("dve", lambda: nc.vector.scalar_tensor_tensor(out=ang[:], in0=kf[:], scalar=-C1, in1=ang[:],
                                                           op0=ALU.mult, op1=ALU.add), reads=[b_kf], writes=[b_ang])
        S.op("dve", lambda: nc.vector.scalar_tensor_tensor(out=ang[:], in0=kf[:], scalar=-C2, in1=ang[:],
                                                           op0=ALU.mult, op1=ALU.add), reads=[b_kf], writes=[b_ang])
        for which, (shift, dstT, b_dst) in enumerate([(0.0, sinT, b_sin), (0.5 * PI, cosT, b_cos)]):
            S.op("dve", lambda: nc.vector.tensor_scalar(out=msk[:], in0=ang[:], scalar1=shift, scalar2=PI,
                                                        op0=ALU.add, op1=ALU.is_gt), reads=[b_ang], writes=[b_msk])
            S.op("dve", lambda: nc.vector.scalar_tensor_tensor(out=arg[:], in0=msk[:], scalar=-2 * PI, in1=ang[:],
                                                               op0=ALU.mult, op1=ALU.add),
                 reads=[b_msk, b_ang], writes=[b_arg])
            S.op("dve", lambda: nc.vector.tensor_scalar(out=arg[:], in0=arg[:], scalar1=shift, scalar2=PI,
                                                        op0=ALU.add, op1=ALU.min), reads=[], writes=[b_arg])
            S.op("dve", lambda: nc.vector.tensor_scalar(out=arg[:], in0=arg[:], scalar1=-PI, scalar2=None,
                                                        op0=ALU.max), reads=[], writes=[b_arg])
            S.op("act", lambda: nc.scalar.activation(out=dstT[:, cs_], in_=arg[:], func=AF.Sin),
                 reads=[b_arg], writes=[b_dst])

    S.op("dve", lambda: nc.vector.tensor_tensor(out=ot[0][:, 0:64], in0=lamv[:, 0, :], in1=lamv[:, 1, :], op=ALU.mult),
         reads=[b_lamv], writes=[b_ot[0]])
    S.op("dve", lambda: nc.vector.tensor_tensor(out=ot[0][:, 64:128], in0=lamv[:, 2, :], in1=lamv[:, 3, :], op=ALU.mult),
         reads=[b_lamv], writes=[b_ot[0]])
    S.op("dve", lambda: nc.vector.tensor_reduce(out=lam[:, 0:1], in_=ot[0][:, 0:64], axis=AX.X, op=ALU.add),
         reads=[b_ot[0]], writes=[b_lam])
    S.op("dve", lambda: nc.vector.tensor_reduce(out=lam[:, 1:2], in_=ot[0][:, 64:128], axis=AX.X, op=ALU.add),
         reads=[b_ot[0]], writes=[b_lam])
    S.op("act", lambda: nc.scalar.activation(out=lam[:, 0:2], in_=lam[:, 0:2], func=AF.Exp), reads=[b_lam], writes=[b_lam])
    S.op("dve", lambda: nc.vector.tensor_tensor(out=lam[:, 2:3], in0=lam[:, 0:1], in1=lam[:, 1:2], op=ALU.subtract),
         reads=[b_lam], writes=[b_lam])
    S.op("dve", lambda: nc.vector.tensor_scalar(out=lam[:, 3:4], in0=lam[:, 2:3], scalar1=float(lambda_init), scalar2=None,
                                                op0=ALU.add), reads=[b_lam], writes=[b_lam])
    lam_ap = lam[:, 3:4]
    S.op("dve", lambda: nc.vector.tensor_scalar(out=cvals[:, 2:3], in0=lam[:, 3:4], scalar1=-1.0, scalar2=None,
                                                op0=ALU.mult), reads=[b_lam], writes=[b_cv])
    neglam_ap = cvals[:, 2:3]
    S.op("dve", lambda: nc.vector.tensor_scalar(out=subln[:], in0=subln[:], scalar1=float(1.0 - lambda_init),
                                                scalar2=None, op0=ALU.mult), reads=[b_subln], writes=[b_subln])

    hv = hT.rearrange("c p t -> p c t")
    eps_ap = cvals[:, 0:1]
    pending = []
    hcount = 0
    for j in range(NH if stage >= 1 else 0):
        ws = j % 2
        S.dma("pool", Wb[ws][:], w[j].rearrange("(kc p) n -> p kc n", p=128), writes=[b_W[ws]])
        W = Wb[ws]
        for tb in range(8):
            hs = hcount % 2
            hcount += 1
            ts = slice(tb * 512, (tb + 1) * 512)
            S.dma("sp", hblk[hs][:], hv[:, :, ts], writes=[b_h[hs]])
            hb = hblk[hs]
            for which in range(2):
                c0 = which * 128
                nw = cst[:, 1 + which:2 + which]
                dst, b_dst = (QT, b_QT) if which == 0 else (KT, b_KT)
                pacc, b_pacc = (pA, b_pA) if which == 0 else (pB, b_pB)
                for kc in range(16):
                    S.op("pe", lambda: nc.tensor.matmul(pacc[:], lhsT=W[:, kc, c0:c0 + 128], rhs=hb[:, kc, :],
                                                        start=(kc == 0), stop=(kc == 15)),
                         reads=[b_W[ws], b_h[hs]], writes=[b_pacc])
                qw, b_qw = bft[which * 2], b_bf[which * 2]
                sq, b_sqq = bft[which * 2 + 1], b_bf[which * 2 + 1]
                S.op("act", lambda: nc.scalar.activation(out=qw[:], in_=pacc[:], func=AF.Identity, scale=nw),
                     reads=[b_pacc, b_cst], writes=[b_qw])
                S.op("act", lambda: nc.scalar.activation(out=sq[:], in_=pacc[:], func=AF.Square),
                     reads=[b_pacc], writes=[b_sqq])
                S.op("pe", lambda: nc.tensor.matmul(pC[:], lhsT=bones, rhs=sq[:], start=True, stop=True),
                     reads=[b_sqq, b_mats], writes=[b_pC])
                S.op("pe", lambda: nc.tensor.matmul(pD[:], lhsT=r0t, rhs=qw[:], start=True, stop=True),
                     reads=[b_qw, b_mats], writes=[b_pD])
                f0, f1, f2, f3 = [f32t[which * 4 + i] for i in range(4)]
                g0, g1, g2, g3 = [b_f[which * 4 + i] for i in range(4)]
                S.op("act", lambda: nc.scalar.activation(out=f0[:], in_=pC[:], func=AF.Sqrt, bias=eps_ap, scale=1.0 / 64),
                     reads=[b_pC, b_cv], writes=[g0])
                S.op("dve", lambda: nc.vector.reciprocal(out=f1[:], in_=f0[:]), reads=[g0], writes=[g1])
                S.op("dve", lambda: nc.vector.scalar_tensor_tensor(out=f2[:], in0=pacc[:], scalar=nw, in1=cosT[:, ts],
                                                                   op0=ALU.mult, op1=ALU.mult),
                     reads=[b_pacc, b_cst, b_cos, b_qw, b_sqq], writes=[g2])
                S.op("dve", lambda: nc.vector.tensor_tensor(out=f3[:], in0=pD[:], in1=sinT[:, ts], op=ALU.mult),
                     reads=[b_pD, b_sin], writes=[g3])
                S.op("pool", lambda: nc.gpsimd.tensor_tensor(out=f2[:], in0=f2[:], in1=f3[:], op=ALU.add),
                     reads=[g3], writes=[g2])
                S.op("pool", lambda: nc.gpsimd.tensor_tensor(out=dst[:, ts], in0=f2[:], in1=f1[:], op=ALU.mult),
                     reads=[g2, g1], writes=[b_dst])
            for tt in range(4):
                tile = tb * 4 + tt
                pacc, b_pacc = (pA, b_pA) if tt % 2 == 0 else (pB, b_pB)
                for kc in range(16):
                    S.op("pe", lambda: nc.tensor.matmul(pacc[:, 0:256], lhsT=hb[:, kc, tt * 128:(tt + 1) * 128],
                                                        rhs=W[:, kc, 256:512], start=(kc == 0), stop=(kc == 15)),
                         reads=[b_W[ws], b_h[hs]], writes=[b_pacc])
                S.op("act", lambda: nc.scalar.activation(out=Vaug[:, tile, 0:128], in_=pacc[:, 0:128], func=AF.Copy),
                     reads=[b_pacc], writes=[b_V])
                gt, b_gt = ot[tt % 2], b_ot[tt % 2]
                S.op("act", lambda: nc.scalar.activation(out=gt[:], in_=pacc[:, 128:256], func=AF.Silu),
                     reads=[b_pacc], writes=[b_gt])
                S.op("dve", lambda: nc.vector.tensor_tensor(out=Gs[:, tile, :], in0=gt[:], in1=subln[:], op=ALU.mult),
                     reads=[b_gt, b_subln], writes=[b_G])
        ys = j % 2
        ycur = yT_sb[ys]
        scount = 0
        for qb in range(8 if stage >= 3 else (1 if stage == 2 else 0)):
            for m in range(2):
                ps_ = slice(m * 64, (m + 1) * 64)
                for kt in range(4 * qb + 4):
                    r = kt - 4 * qb
                    q0 = max(r, 0) * 128
                    n = 512 - q0
                    pS, b_pS = (pA, b_pA) if scount % 2 == 0 else (pB, b_pB)
                    pt, b_pt = pTs[scount % 3], b_pTs[scount % 3]
                    scount += 1
                    S.op("pe", lambda: nc.tensor.matmul(pS[:, 0:n], lhsT=KT[ps_, kt * 128:(kt + 1) * 128],
                                                        rhs=QT[ps_, qb * 512 + q0:qb * 512 + 512], start=True, stop=True),
                         reads=[b_KT, b_QT], writes=[b_pS])
                    S.op("act", lambda: nc.scalar.activation(out=pt[:, 0:n], in_=pS[:, 0:n], func=AF.Exp, scale=0.125),
                         reads=[b_pS], writes=[b_pt])
                    if r >= 0:
                        S.op("pool", lambda: nc.gpsimd.tensor_tensor(out=pt[:, 0:128], in0=pt[:, 0:128], in1=masktri,
                                                                     op=ALU.mult),
                             reads=[b_mats], writes=[b_pt])
                    for qt in range(max(r, 0), 4 if int(os.environ.get('SUB', '9')) >= 1 else 0):
                        S.op("pe", lambda: nc.tensor.matmul(acc(m, qt), lhsT=pt[:, qt * 128 - q0:qt * 128 - q0 + 128],
                                                            rhs=Vaug[:, kt, 0:129], start=(kt == 0), stop=(kt == 4 * qb + qt)),
                             reads=[b_pt, b_V, b_Vones], writes=[b_pO[qt]])
                if m == 0:
                    for qt in range(4):
                        e0 = qt % 2
                        a1 = acc(0, qt)
                        S.op("dve", lambda: nc.vector.reciprocal(out=sm[:, 12 + qt:13 + qt], in_=a1[:, 128:129]),
                             reads=[b_pO[qt]], writes=[b_sm[2 + qt]])
                        S.op("dve", lambda: nc.vector.tensor_scalar(out=o1s[:, qt, :], in0=a1[:, 0:128],
                                                                    scalar1=sm[:, 12 + qt:13 + qt], scalar2=None, op0=ALU.mult),
                             reads=[b_pO[qt], b_sm[2 + qt]], writes=[b_o1s[qt]])
            for fn in pending:
                fn()
            pending = []
            for qt in range(4 if int(os.environ.get('SUB', '9')) >= 2 else 0):
                tile = qb * 4 + qt
                e = tile % 2
                a2 = acc(1, qt)
                sm_ = sm[:, e * 6:e * 6 + 6]
                bs = b_sm[e]
                S.op("dve", lambda: nc.vector.reciprocal(out=sm_[:, 1:2], in_=a2[:, 128:129]), reads=[b_pO[qt]], writes=[bs])
                S.op("dve", lambda: nc.vector.tensor_tensor(out=sm_[:, 2:3], in0=sm_[:, 1:2], in1=neglam_ap, op=ALU.mult),
                     reads=[b_cv], writes=[bs])
                S.op("dve", lambda: nc.vector.scalar_tensor_tensor(out=ot[e][:], in0=a2[:, 0:128], scalar=sm_[:, 2:3],
                                                                   in1=o1s[:, qt, :], op0=ALU.mult, op1=ALU.add),
                     reads=[b_pO[qt], bs, b_o1s[qt]], writes=[b_ot[e]])
                S.op("pool", lambda: nc.gpsimd.tensor_tensor(out=sqt[e][:], in0=ot[e][:], in1=ot[e][:], op=ALU.mult),
                     reads=[b_ot[e]], writes=[b_sq[e]])
                S.op("dve", lambda: nc.vector.tensor_reduce(out=sm_[:, 3:4], in_=sqt[e][:], axis=AX.X, op=ALU.add),
                     reads=[b_sq[e]], writes=[bs])
                S.op("act", lambda: nc.scalar.activation(out=sm_[:, 4:5], in_=sm_[:, 3:4], func=AF.Sqrt, bias=eps_ap,
                                                         scale=1.0 / 128), reads=[bs, b_cv], writes=[bs])
                S.op("dve", lambda: nc.vector.reciprocal(out=sm_[:, 5:6], in_=sm_[:, 4:5]), reads=[bs], writes=[bs])
                S.op("dve", lambda: nc.vector.scalar_tensor_tensor(out=ybf[qt], in0=ot[e][:], scalar=sm_[:, 5:6],
                                                                   in1=Gs[:, tile, :], op0=ALU.mult, op1=ALU.mult),
                     reads=[b_ot[e], bs, b_G], writes=[b_yb[qt]])

                def tr(e=e, tile=tile, ycur=ycur, ys=ys, qt=qt):
                    if os.environ.get('NOPE') is None:
                        S.op("pe", lambda: nc.tensor.matmul(pT[:, e * 128:(e + 1) * 128], lhsT=ybf[qt], rhs=ident,
                                                            start=True, stop=True),
                             reads=[b_yb[qt], b_mats], writes=[b_pT[e]])
                    S.op("dve", lambda: nc.vector.tensor_copy(out=ycur[:, tile * 128:(tile + 1) * 128],
                                                              in_=pT[:, e * 128:(e + 1) * 128]),
                         reads=[b_pT[e]], writes=[b_yT[ys]])
                if int(os.environ.get('SUB', '9')) >= 3:
                    pending.append(tr)
        for fn in pending:
            fn()
        pending = []
        S.dma("sp", yT[j], ycur[:], reads=[b_yT[ys]])
    S.finish()
    return nc


def attn_inputs(hT_b, pos_b, w_in, qn, kn, lq1, lk1, lq2, lk2, subln, hg):
    heads = range(hg * 4, hg * 4 + 4)
    ws = []
    for hd in heads:
        cols = []
        for blk in range(4):
            cols.append(w_in[:, blk * 2048 + hd * 128: blk * 2048 + (hd + 1) * 128])
        ws.append(np.concatenate(cols, axis=1))
    w = np.ascontiguousarray(np.stack(ws, axis=0))
    inv_freq = (10000.0 ** (-np.arange(0, 64, 2, dtype=np.float32) / np.float32(64))).astype(np.float32)
    p = np.arange(128)
    cst = np.zeros((128, 4), np.float32)
    cst[:, 0] = inv_freq[p % 32]
    cst[:, 1] = qn[p % 64]
    cst[:, 2] = kn[p % 64]
    lamv = np.ascontiguousarray(np.broadcast_to(np.stack([lq1, lk1, lq2, lk2], 0)[None], (128, 4, 64))).astype(np.float32)
    return {
        "hT": hT_b, "w": w,
        "pos": np.ascontiguousarray(np.broadcast_to(pos_b[None, :], (128, SEQ))).astype(np.int32),
        "cst": cst, "lamv": lamv,
        "subln": np.ascontiguousarray(np.broadcast_to(subln[None, :], (128, 128))).astype(np.float32),
        "mats": attn_consts(),
    }


def ssd_consts():
    k = np.arange(128)
    tri = (k[:, None] <= k[None, :]).astype(np.float32)
    ones = np.ones((128, 128), np.float32)
    ident = np.eye(128, dtype=np.float32)
    mats = np.ascontiguousarray(np.stack([tri, ones, ident], axis=1))
    diag = np.zeros((128, 8, 128), np.float32)
    for h in range(8):
        diag[h, h, :] = 1.0
    return mats, diag


def build_ssd():
    nc = _new_nc()
    S = Sched(nc)
    hT = nc.dram_tensor("hT", [16, 128, SEQ], BF16, kind="ExternalInput").ap()
    w = nc.dram_tensor("w", [2, D, 1288], F32, kind="ExternalInput").ap()
    convw_d = nc.dram_tensor("convw", [2, 128, 6, 4], F32, kind="ExternalInput").ap()
    convb_d = nc.dram_tensor("convb", [2, 128, 6], F32, kind="ExternalInput").ap()
    hp_d = nc.dram_tensor("hp", [2, 128, 24], F32, kind="ExternalInput").ap()
    normw_d = nc.dram_tensor("normw", [2, 128, 512], F32, kind="ExternalInput").ap()
    mats_d = nc.dram_tensor("mats", [128, 3, 128], F32, kind="ExternalInput").ap()
    diag_d = nc.dram_tensor("diag", [128, 8, 128], F32, kind="ExternalInput").ap()
    yT = nc.dram_tensor("yT", [8, 128, SEQ], BF16, kind="ExternalOutput").ap()

    def sb(name, shape, dt):
        return nc.alloc_sbuf_tensor("s_" + name, shape, dt)

    matsf = sb("matsf", [128, 3, 128], F32)
    matsb = sb("matsb", [128, 3, 128], BF16)
    diag = sb("diag", [128, 8, 128], F32)
    Wb = sb("Wb", [128, 16, 1536], BF16)
    hblk = [sb("hblk%d" % i, [128, 16, 512], BF16) for i in range(2)]
    convw = sb("convw", [128, 6, 4], F32)
    convb = sb("convb", [128, 6], F32)
    hp = sb("hp", [128, 24], F32)
    normw = sb("normw", [128, 512], F32)
    Dvec = sb("Dvec", [128, 8, 64], F32)
    Aneg = sb("Aneg", [128, 8], F32)
    cvals = sb("cvals", [128, 4], F32)
    pre = sb("pre", [128, 6, 516], F32)
    cacc = [sb("cacc%d" % i, [128, 512], F32) for i in range(2)]
    xsT = [sb("xsT%d" % i, [128, 512], BF16) for i in range(2)]
    BT = sb("BT", [128, 512], BF16)
    CT = sb("CT", [128, 512], BF16)
    xs_tok = sb("xs_tok", [128, 4, 512], BF16)
    B_tok = sb("B_tok", [128, 4, 128], BF16)
    zs = sb("zs", [128, 512], F32)
    sm = sb("sm", [128, 128], F32)
    csT_sb = sb("csT_sb", [128, 128], F32)
    rhsD = sb("rhsD", [128, 3, 8, 128], BF16)
    ab = sb("ab", [128, 512], BF16)
    b_ab = S.buf()
    segt = sb("segt", [128, 8, 128], F32)
    ecsb = sb("ecsb", [128, 8, 128], F32)
    CBm = sb("CBm", [128, 128], F32)
    MT = sb("MT", [128, 8, 128], BF16)
    CsT = sb("CsT", [128, 8, 128], BF16)
    xdt = sb("xdt", [128, 8, 128], BF16)
    xds = sb("xds", [128, 8, 64], BF16)
    prevT = sb("prevT", [128, 8, 64], F32)
    prevb = sb("prevb", [128, 8, 128], BF16)
    tst = sb("tst", [128, 8, 64], F32)
    t1 = sb("t1", [128, 512], F32)
    t2 = sb("t2", [128, 512], F32)
    t3 = sb("t3", [128, 512], F32)
    ynb = sb("ynb", [128, 512], BF16)
    yT_sb = sb("yT_sb", [128, 4, 512], BF16)

    pA = nc.alloc_psum_tensor("pA", [128, 512], F32)
    pB = nc.alloc_psum_tensor("pB", [128, 512], F32)
    pSm = nc.alloc_psum_tensor("pSm", [128, 512], F32)
    pCs = [nc.alloc_psum_tensor("pCs%d" % i, [128, 512], F32) for i in range(2)]
    pY = nc.alloc_psum_tensor("pY", [128, 512], F32)
    pSt = nc.alloc_psum_tensor("pSt", [128, 512], F32)
    pT = nc.alloc_psum_tensor("pT", [128, 512], F32)

    (b_mf, b_mb, b_diag, b_W, b_cw, b_cb, b_hp, b_nw, b_Dv, b_An, b_cv, b_BT, b_CT, b_xs, b_Bt, b_zs,
     b_csT, b_rhsD, b_seg, b_ecs, b_CBm, b_MT, b_CsT, b_xdt, b_xds, b_prev, b_prevb, b_tst, b_t1, b_t2, b_t3,
     b_ynb, b_yT, b_pA, b_pB, b_pY, b_pSt, b_pT) = S.bufs(38)
    b_h = S.bufs(2)
    b_pre = S.bufs(6)
    b_cacc = S.bufs(2)
    b_xsT = S.bufs(2)
    b_pCs = S.bufs(2)
    b_sm = S.bufs(16)
    b_pSm = S.bufs(5)

    S.dma("sp", matsf[:], mats_d, writes=[b_mf])
    S.dma("pool", matsb[:], mats_d, writes=[b_mb])
    S.dma("sp", diag[:], diag_d, writes=[b_diag])
    tri_f, ones_f = matsf[:, 0, :], matsf[:, 1, :]
    tri_b, ones_b = matsb[:, 0, :], matsb[:, 1, :]
    ident_b = matsb[:, 2, :]
    S.op("dve", lambda: nc.vector.memset(cvals[:, 0:1], EPS), writes=[b_cv])
    S.op("dve", lambda: nc.vector.memset(cvals[:, 1:2], 1.0), writes=[b_cv])
    S.op("pool", lambda: nc.gpsimd.memset(xdt[:], 0.0), writes=[b_xdt])
    hv = hT.rearrange("c p t -> p c t")
    hcount = 0

    def SM(i):
        return sm[:, i * 8:(i + 1) * 8]

    for gi in range(2):
        S.dma("pool", Wb[:, :, 0:1288], w[gi].rearrange("(kc p) n -> p kc n", p=128), writes=[b_W])
        S.dma("sp", convw[:], convw_d[gi], writes=[b_cw])
        S.dma("sp", convb[:], convb_d[gi], writes=[b_cb])
        S.dma("sp", hp[:], hp_d[gi], writes=[b_hp])
        S.dma("sp", normw[:], normw_d[gi], writes=[b_nw])
        S.op("act", lambda: nc.scalar.activation(out=Aneg[:], in_=hp[:, 8:16], func=AF.Exp), reads=[b_hp], writes=[b_An])
        S.op("dve", lambda: nc.vector.tensor_scalar(out=Aneg[:], in0=Aneg[:], scalar1=-1.0, scalar2=None, op0=ALU.mult),
             writes=[b_An])
        S.op("dve", lambda: nc.vector.tensor_copy(out=Dvec[:], in_=hp[:, 16:24].unsqueeze(2).to_broadcast([128, 8, 64])),
             reads=[b_hp], writes=[b_Dv])
        S.op("pool", lambda: nc.gpsimd.memset(pre[:, :, 0:3], 0.0), writes=b_pre)
        S.op("pool", lambda: nc.gpsimd.memset(prevT[:], 0.0), writes=[b_prev])
        S.op("pool", lambda: nc.gpsimd.memset(prevb[:], 0.0), writes=[b_prevb])
        for tb in range(int(os.environ.get('NTB', '8'))):
            hs = hcount % 2
            hcount += 1
            ts = slice(tb * 512, (tb + 1) * 512)
            S.dma("sp", hblk[hs][:], hv[:, :, ts], writes=[b_h[hs]])
            hb = hblk[hs]
            for cc in range(6):
                pacc, b_pacc = (pA, b_pA) if cc % 2 == 0 else (pB, b_pB)
                for kc in range(16):
                    S.op("pe", lambda: nc.tensor.matmul(pacc[:], lhsT=Wb[:, kc, cc * 128:(cc + 1) * 128], rhs=hb[:, kc, :],
                                                        start=(kc == 0), stop=(kc == 15)),
                         reads=[b_W, b_h[hs]], writes=[b_pacc])
                S.op("act", lambda: nc.scalar.activation(out=pre[:, cc, 3:515], in_=pacc[:], func=AF.Copy),
                     reads=[b_pacc], writes=[b_pre[cc]])
                ca, b_ca = cacc[cc % 2], b_cacc[cc % 2]
                S.op("dve", lambda: nc.vector.tensor_scalar(out=ca[:], in0=pre[:, cc, 0:512], scalar1=convw[:, cc, 0:1],
                                                            scalar2=None, op0=ALU.mult),
                     reads=[b_pre[cc], b_cw], writes=[b_ca])
                for k in range(1, 4):
                    S.op("dve", lambda: nc.vector.scalar_tensor_tensor(out=ca[:], in0=pre[:, cc, k:k + 512],
                                                                       scalar=convw[:, cc, k:k + 1], in1=ca[:],
                                                                       op0=ALU.mult, op1=ALU.add),
                         reads=[b_pre[cc], b_cw], writes=[b_ca])
                if cc < 4:
                    dst, b_dst = xsT[cc % 2], b_xsT[cc % 2]
                elif cc == 4:
                    dst, b_dst = BT, b_BT
                else:
                    dst, b_dst = CT, b_CT
                S.op("act", lambda: nc.scalar.activation(out=dst[:], in_=ca[:], func=AF.Silu, bias=convb[:, cc:cc + 1]),
                     reads=[b_ca, b_cb], writes=[b_dst])
                S.op("pool", lambda: nc.gpsimd.tensor_copy(out=pre[:, cc, 0:3], in_=pre[:, cc, 512:515]),
                     writes=[b_pre[cc]])
                if cc < 5 and os.environ.get('NOTR') is None:
                    for tt in range(4):
                        S.op("pe", lambda: nc.tensor.matmul(pT[:, tt * 128:(tt + 1) * 128], lhsT=dst[:, tt * 128:(tt + 1) * 128],
                                                            rhs=ident_b, start=True, stop=True),
                             reads=[b_dst, b_mb], writes=[b_pT])
                    if cc < 4:
                        for tt in range(4):
                            S.op("dve", lambda: nc.vector.tensor_copy(out=xs_tok[:, tt, cc * 128:(cc + 1) * 128],
                                                                      in_=pT[:, tt * 128:(tt + 1) * 128]),
                                 reads=[b_pT], writes=[b_xs])
                    else:
                        S.op("dve", lambda: nc.vector.tensor_copy(out=B_tok[:].rearrange("p a b -> p (a b)"), in_=pT[:]),
                             reads=[b_pT], writes=[b_Bt])
            for c in range(4):
                if int(os.environ.get('SST', '9')) < 1:
                    continue
                cs_ = slice(c * 128, (c + 1) * 128)
                for kc in range(16):
                    S.op("pe", lambda: nc.tensor.matmul(pA[:], lhsT=hb[:, kc, cs_], rhs=Wb[:, kc, 768:1280],
                                                        start=(kc == 0), stop=(kc == 15)),
                         reads=[b_W, b_h[hs]], writes=[b_pA])
                for kc in range(16):
                    S.op("pe", lambda: nc.tensor.matmul(pSm[:, 0:8], lhsT=hb[:, kc, cs_], rhs=Wb[:, kc, 1280:1288],
                                                        start=(kc == 0), stop=(kc == 15)),
                         reads=[b_W, b_h[hs]], writes=[b_pSm[0]])
                S.op("act", lambda: nc.scalar.activation(out=zs[:], in_=pA[:], func=AF.Silu), reads=[b_pA], writes=[b_zs])
                if int(os.environ.get('SSUB', '9')) < 1:
                    continue
                S.op("dve", lambda: nc.vector.tensor_tensor(out=SM(0), in0=pSm[:, 0:8], in1=hp[:, 0:8], op=ALU.add),
                     reads=[b_pSm[0], b_hp], writes=[b_sm[0]])
                S.op("dve", lambda: nc.vector.tensor_scalar(out=SM(11), in0=SM(0), scalar1=-1.0, scalar2=None, op0=ALU.mult),
                     reads=[b_sm[0]], writes=[b_sm[14]])
                S.op("dve", lambda: nc.vector.tensor_tensor(out=SM(1), in0=SM(0), in1=SM(11), op=ALU.max),
                     reads=[b_sm[0], b_sm[14]], writes=[b_sm[1]])
                S.op("act", lambda: nc.scalar.activation(out=SM(2), in_=SM(1), func=AF.Exp, scale=-1.0),
                     reads=[b_sm[1]], writes=[b_sm[2]])
                S.op("act", lambda: nc.scalar.activation(out=SM(3), in_=SM(2), func=AF.Ln, bias=cvals[:, 1:2]),
                     reads=[b_sm[2], b_cv], writes=[b_sm[3]])
                S.op("dve", lambda: nc.vector.tensor_scalar(out=SM(4), in0=SM(0), scalar1=0.0, scalar2=None, op0=ALU.max),
                     reads=[b_sm[0]], writes=[b_sm[4]])
                S.op("dve", lambda: nc.vector.tensor_tensor(out=SM(5), in0=SM(4), in1=SM(3), op=ALU.add),
                     reads=[b_sm[4], b_sm[3]], writes=[b_sm[5]])
                S.op("dve", lambda: nc.vector.tensor_tensor(out=SM(6), in0=SM(5), in1=Aneg[:], op=ALU.mult),
                     reads=[b_sm[5], b_An], writes=[b_sm[6]])
                if int(os.environ.get('SSUB', '9')) < 2:
                    continue
                S.op("dve", lambda: nc.vector.tensor_copy(out=ab[:, 0:8], in_=SM(6)), reads=[b_sm[6]], writes=[b_ab])
                S.op("dve", lambda: nc.vector.tensor_tensor(out=SM(12), in0=SM(6), in1=ab[:, 0:8], op=ALU.subtract),
                     reads=[b_sm[6], b_ab], writes=[b_sm[15]])
                S.op("dve", lambda: nc.vector.tensor_copy(out=ab[:, 8:16], in_=SM(12)), reads=[b_sm[15]], writes=[b_ab])
                S.op("dve", lambda: nc.vector.tensor_tensor(out=SM(12), in0=SM(12), in1=ab[:, 8:16], op=ALU.subtract),
                     reads=[b_ab], writes=[b_sm[15]])
                S.op("dve", lambda: nc.vector.tensor_copy(out=ab[:, 16:24], in_=SM(12)), reads=[b_sm[15]], writes=[b_ab])
                for j3 in range(3):
                    S.op("pe", lambda: nc.tensor.matmul(pSm[:, 8:16], lhsT=tri_b, rhs=ab[:, j3 * 8:(j3 + 1) * 8],
                                                        start=(j3 == 0), stop=(j3 == 2)),
                         reads=[b_ab, b_mb], writes=[b_pSm[1]])
                for j3 in range(3):
                    S.op("pe", lambda: nc.tensor.matmul(pSm[:, 16:24], lhsT=ones_b, rhs=ab[:, j3 * 8:(j3 + 1) * 8],
                                                        start=(j3 == 0), stop=(j3 == 2)),
                         reads=[b_ab, b_mb], writes=[b_pSm[2]])
                if int(os.environ.get('SSUB', '9')) < 3:
                    continue
                S.op("dve", lambda: nc.vector.tensor_copy(out=SM(7), in_=pSm[:, 8:16]),
                     reads=[b_pSm[1]], writes=[b_sm[7]])
                S.op("dve", lambda: nc.vector.tensor_tensor(out=SM(8), in0=pSm[:, 16:24], in1=SM(7), op=ALU.subtract),
                     reads=[b_pSm[2], b_sm[7]], writes=[b_sm[8]])
                S.op("act", lambda: nc.scalar.activation(out=SM(9), in_=SM(8), func=AF.Exp), reads=[b_sm[8]], writes=[b_sm[9]])
                S.op("dve", lambda: nc.vector.tensor_tensor(out=SM(10), in0=SM(5), in1=SM(9), op=ALU.mult),
                     reads=[b_sm[5], b_sm[9]], writes=[b_sm[10]])
                if int(os.environ.get('SST', '9')) < 2:
                    continue
                for j3 in range(3):
                    S.op("dve", lambda: nc.vector.tensor_tensor(
                        out=rhsD[:, j3, :, :], in0=tri_b.unsqueeze(1).to_broadcast([128, 8, 128]),
                        in1=ab[:, j3 * 8:(j3 + 1) * 8].unsqueeze(2).to_broadcast([128, 8, 128]), op=ALU.mult),
                         reads=[b_ab, b_mb], writes=[b_rhsD])
                if int(os.environ.get('S2', '9')) < 1:
                    continue
                for half in range(2):
                    for j3 in range(3):
                        S.op("pe", lambda: nc.tensor.matmul(
                            pCs[half][:], lhsT=ones_b,
                            rhs=rhsD[:, j3, half * 4:(half + 1) * 4, :].rearrange("p a b -> p (a b)"),
                            start=(j3 == 0), stop=(j3 == 2)),
                             reads=[b_rhsD, b_mb], writes=[b_pCs[half]])
                if int(os.environ.get('S2', '9')) < 2:
                    continue
                for half in range(2):
                    hsl = slice(half * 4, (half + 1) * 4)
                    for h4 in range(0 if os.environ.get('NOSEG') else 4):
                        hh = half * 4 + h4
                        S.op("dve", lambda: nc.vector.tensor_scalar(out=segt[:, hh, :], in0=pCs[half][:, h4 * 128:(h4 + 1) * 128],
                                                                    scalar1=sm[:, 56 + hh:57 + hh], scalar2=0.0,
                                                                    op0=ALU.subtract, op1=ALU.min),
                             reads=[b_pCs[half], b_sm[7]], writes=[b_seg])
                    if os.environ.get('NOECS') is None:
                        S.op("act", lambda: nc.scalar.activation(out=ecsb[:, hsl, :].rearrange("p a b -> p (a b)"),
                                                                 in_=pCs[half][:], func=AF.Exp),
                             reads=[b_pCs[half], b_seg], writes=[b_ecs])
                if int(os.environ.get('S2', '9')) < 3:
                    continue
                S.op("act", lambda: nc.scalar.activation(out=segt[:].rearrange("p a b -> p (a b)"),
                                                         in_=segt[:].rearrange("p a b -> p (a b)"), func=AF.Exp), writes=[b_seg])
                if int(os.environ.get('SST', '9')) < 3:
                    continue
                S.op("pe", lambda: nc.tensor.matmul(pSm[:, 256:384], lhsT=BT[:, cs_], rhs=CT[:, cs_], start=True, stop=True),
                     reads=[b_BT, b_CT], writes=[b_pSm[4]])
                S.op("dve", lambda: nc.vector.tensor_tensor(out=CBm[:], in0=pSm[:, 256:384], in1=tri_f, op=ALU.mult),
                     reads=[b_pSm[4], b_mf], writes=[b_CBm])
                S.op("pool", lambda: nc.gpsimd.tensor_tensor(out=MT[:], in0=segt[:],
                                                             in1=CBm[:].unsqueeze(1).to_broadcast([128, 8, 128]), op=ALU.mult),
                     reads=[b_seg, b_CBm], writes=[b_MT])
                S.op("dve", lambda: nc.vector.tensor_tensor(out=CsT[:], in0=ecsb[:],
                                                            in1=CT[:, cs_].unsqueeze(1).to_broadcast([128, 8, 128]), op=ALU.mult),
                     reads=[b_ecs, b_CT], writes=[b_CsT])
                xs3 = xs_tok[:, c, :].rearrange("p (a b) -> p a b", a=8)
                S.op("dve", lambda: nc.vector.tensor_tensor(out=xdt[:, :, 0:64], in0=xs3,
                                                            in1=SM(5).unsqueeze(2).to_broadcast([128, 8, 64]), op=ALU.mult),
                     reads=[b_xs, b_sm[5]], writes=[b_xdt])
                S.op("pool", lambda: nc.gpsimd.tensor_tensor(out=xds[:], in0=xs3,
                                                             in1=SM(10).unsqueeze(2).to_broadcast([128, 8, 64]), op=ALU.mult),
                     reads=[b_xs, b_sm[10]], writes=[b_xds])
                for h in range(8):
                    S.op("pe", lambda: nc.tensor.matmul(pY[:, h * 64:(h + 1) * 64], lhsT=MT[:, h, :], rhs=xdt[:, h, 0:64],
                                                        start=True, stop=False),
                         reads=[b_MT, b_xdt], writes=[b_pY])
                    S.op("pe", lambda: nc.tensor.matmul(pY[:, h * 64:(h + 1) * 64], lhsT=CsT[:, h, :], rhs=prevb[:, h, 0:64],
                                                        start=False, stop=True),
                         reads=[b_CsT, b_prevb], writes=[b_pY])
                S.op("pe", lambda: nc.tensor.matmul(pSt[:], lhsT=B_tok[:, c, :], rhs=xds[:].rearrange("p a b -> p (a b)"),
                                                    start=True, stop=True),
                     reads=[b_Bt, b_xds], writes=[b_pSt])
                if int(os.environ.get('SST', '9')) < 4:
                    continue
                S.op("dve", lambda: nc.vector.tensor_tensor(out=tst[:], in0=prevT[:],
                                                            in1=ecsb[:, :, 127:128].to_broadcast([128, 8, 64]), op=ALU.mult),
                     reads=[b_prev, b_ecs], writes=[b_tst])
                S.op("dve", lambda: nc.vector.tensor_tensor(out=prevT[:], in0=tst[:],
                                                            in1=pSt[:].rearrange("p (a b) -> p a b", a=8), op=ALU.add),
                     reads=[b_tst, b_pSt], writes=[b_prev])
                S.op("pool", lambda: nc.gpsimd.tensor_copy(out=prevb[:, :, 0:64], in_=prevT[:]), reads=[b_prev], writes=[b_prevb])
                if int(os.environ.get('SST', '9')) < 5:
                    continue
                S.op("pool", lambda: nc.gpsimd.tensor_tensor(out=t1[:], in0=xs_tok[:, c, :], in1=Dvec[:].rearrange("p a b -> p (a b)"),
                                                             op=ALU.mult), reads=[b_xs, b_Dv], writes=[b_t1])
                S.op("dve", lambda: nc.vector.tensor_tensor(out=t2[:], in0=pY[:], in1=t1[:], op=ALU.add),
                     reads=[b_pY, b_t1], writes=[b_t2])
                S.op("pool", lambda: nc.gpsimd.tensor_tensor(out=t2[:], in0=t2[:], in1=zs[:], op=ALU.mult),
                     reads=[b_zs], writes=[b_t2])
                S.op("pool", lambda: nc.gpsimd.tensor_tensor(out=t3[:], in0=t2[:], in1=t2[:], op=ALU.mult),
                     reads=[b_t2], writes=[b_t3])
                S.op("dve", lambda: nc.vector.tensor_reduce(out=sm[:, 100:101], in_=t3[:], axis=AX.X, op=ALU.add),
                     reads=[b_t3], writes=[b_sm[11]])
                S.op("act", lambda: nc.scalar.activation(out=sm[:, 101:102], in_=sm[:, 100:101], func=AF.Sqrt, bias=cvals[:, 0:1],
                                                         scale=1.0 / 512), reads=[b_sm[11], b_cv], writes=[b_sm[12]])
                S.op("dve", lambda: nc.vector.reciprocal(out=sm[:, 102:103], in_=sm[:, 101:102]), reads=[b_sm[12]], writes=[b_sm[13]])
                S.op("dve", lambda: nc.vector.scalar_tensor_tensor(out=ynb[:], in0=t2[:], scalar=sm[:, 102:103], in1=normw[:],
                                                                   op0=ALU.mult, op1=ALU.mult),
                     reads=[b_t2, b_sm[13], b_nw], writes=[b_ynb])
                for cc in range(4):
                    S.op("pe", lambda: nc.tensor.matmul(pT[:, cc * 128:(cc + 1) * 128], lhsT=ynb[:, cc * 128:(cc + 1) * 128],
                                                        rhs=ident_b, start=True, stop=True),
                         reads=[b_ynb, b_mb], writes=[b_pT])
                for cc in range(4):
                    S.op("dve", lambda: nc.vector.tensor_copy(out=yT_sb[:, cc, cs_], in_=pT[:, cc * 128:(cc + 1) * 128]),
                         reads=[b_pT], writes=[b_yT])
            S.dma("sp", yT[gi * 4:(gi + 1) * 4].rearrange("c p t -> p c t")[:, :, ts], yT_sb[:], reads=[b_yT])
    S.finish()
    return nc


def ssd_inputs(hT_b, w_in, conv_w, conv_b, dt_bias, A_log, Dp, norm_w, hg):
    ws, cws, cbs, hps, nws = [], [], [], [], []
    for gi in range(2):
        g = hg * 2 + gi
        xcols = w_in[:, 4096 + g * 512: 4096 + (g + 1) * 512]
        bcols = w_in[:, 8192 + g * 128: 8192 + (g + 1) * 128]
        ccols = w_in[:, 9216 + g * 128: 9216 + (g + 1) * 128]
        zcols = w_in[:, g * 512:(g + 1) * 512]
        dcols = w_in[:, 10240 + g * 8: 10240 + (g + 1) * 8]
        ws.append(np.concatenate([xcols, bcols, ccols, zcols, dcols], axis=1))
        ch = np.concatenate([np.arange(g * 512, (g + 1) * 512), 4096 + np.arange(g * 128, (g + 1) * 128),
                             5120 + np.arange(g * 128, (g + 1) * 128)])
        cw = conv_w[:, ch]
        cws.append(cw.T.reshape(6, 128, 4).transpose(1, 0, 2))
        cbs.append(conv_b[ch].reshape(6, 128).T)
        hv = np.concatenate([dt_bias[g * 8:(g + 1) * 8], A_log[g * 8:(g + 1) * 8], Dp[g * 8:(g + 1) * 8]])
        hps.append(np.broadcast_to(hv[None, :], (128, 24)))
        nws.append(np.broadcast_to(norm_w[g * 512:(g + 1) * 512][None, :], (128, 512)))
    mats, diag = ssd_consts()
    f = lambda a: np.ascontiguousarray(np.stack(a, 0)).astype(np.float32)
    return {"hT": hT_b, "w": f(ws), "convw": f(cws), "convb": f(cbs), "hp": f(hps), "normw": f(nws),
            "mats": mats, "diag": diag}


def _gather_hT(h_parts):
    out = []
    for b in range(B):
        out.append(np.ascontiguousarray(np.concatenate([h_parts[b * 4 + q] for q in range(4)], axis=2)))
    return out


def _scatter_yT(y_parts, nch):
    outs = []
    for b in range(B):
        full = np.concatenate([y_parts[b * 4 + hg] for hg in range(4)], axis=0)
        for q in range(4):
            outs.append(np.ascontiguousarray(full[:, :, q * TOK:(q + 1) * TOK]))
    return outs


def kernel(x, c, positions, norm_w, ada_w, ada_b,
           attn_w_in, attn_q_norm, attn_k_norm,
           attn_lambda_q1, attn_lambda_k1, attn_lambda_q2, attn_lambda_k2,
           attn_subln_w, attn_w_out,
           ssm_w_in, ssm_conv_w, ssm_conv_b, ssm_dt_bias, ssm_A_log, ssm_D,
           ssm_norm_w, ssm_w_out):
    f = lambda a: np.asarray(a)
    x, c, positions = f(x).astype(np.float32), f(c).astype(np.float32), f(positions).astype(np.int32)
    mod = run_mod(c, f(ada_w), f(ada_b))
    xs = to_featmajor(x)
    for layer in range(DEPTH):
        j = layer // 2
        modv = [modv_layout(mod[layer][i // 4]) for i in range(NCORE)]
        normw = np.ascontiguousarray(f(norm_w)[layer].reshape(16, 128).T)
        if layer == 0:
            res = _run(build_norm(), [{"xT": xs[i], "modv": modv[i], "normw": normw} for i in range(NCORE)])
            hparts = [res[i]["hT"] for i in range(NCORE)]
        hT = _gather_hT(hparts)
        if layer % 2 == 0:
            li = 0.8 - 0.6 * math.exp(-0.3 * layer)
            in_maps = [attn_inputs(hT[i // 4], positions[i // 4], f(attn_w_in)[j], f(attn_q_norm)[j], f(attn_k_norm)[j],
                                   f(attn_lambda_q1)[j], f(attn_lambda_k1)[j], f(attn_lambda_q2)[j], f(attn_lambda_k2)[j],
                                   f(attn_subln_w)[j], i % 4) for i in range(NCORE)]
            res = _run(build_attn(li), in_maps)
            yparts = _scatter_yT([res[i]["yT"] for i in range(NCORE)], 4)
            wout, WC = f(attn_w_out)[j], 16
        else:
            in_maps = [ssd_inputs(hT[i // 4], f(ssm_w_in)[j], f(ssm_conv_w)[j], f(ssm_conv_b)[j], f(ssm_dt_bias)[j],
                                  f(ssm_A_log)[j], f(ssm_D)[j], f(ssm_norm_w)[j], i % 4) for i in range(NCORE)]
            res = _run(build_ssd(), in_maps)
            yparts = _scatter_yT([res[i]["yT"] for i in range(NCORE)], 8)
            wout, WC = f(ssm_w_out)[j], 32
        wout = np.ascontiguousarray(wout.astype(np.float32))
        if layer + 1 < DEPTH:
            normw2 = np.ascontiguousarray(f(norm_w)[layer + 1].reshape(16, 128).T)
            res = _run(build_outproj(WC, True),
                       [{"yT": yparts[i], "wout": wout, "xT": xs[i], "modv": modv[i],
                         "modv2": modv_layout(mod[layer + 1][i // 4]), "normw": normw2} for i in range(NCORE)])
            hparts = [res[i]["hT"] for i in range(NCORE)]
        else:
            res = _run(build_outproj(WC), [{"yT": yparts[i], "wout": wout, "xT": xs[i], "modv": modv[i]} for i in range(NCORE)])
        xs = [res[i]["xo"] for i in range(NCORE)]
    return from_featmajor(xs)
```
